# Optimizing a Trainium2 kernel written in Bass

```python
import math
import jax, jax.numpy as jnp
from jax import lax
import numpy as np

D_MODEL = 1024
BATCH = 8
SEQ = 2048
DEPTH = 1

CHUNK = 64
N_MEM = 256
EPS = 1e-6

ATT_HEADS = 8
ATT_KV_HEADS = 2
ATT_HEAD_DIM = 64
ATT_GROUP = ATT_HEADS // ATT_KV_HEADS
WINDOW = 128
LOOKBACK = WINDOW // CHUNK
ATT_Q_W = ATT_HEADS * ATT_HEAD_DIM
ATT_KV_W = ATT_KV_HEADS * ATT_HEAD_DIM

HG_HEADS = 4
HG_KEY_DIM = 128
HG_VAL_DIM = 128
HG_K_W = HG_HEADS * HG_KEY_DIM
HG_V_W = HG_HEADS * HG_VAL_DIM

X_HEADS = 4
X_HEAD_DIM = D_MODEL // X_HEADS

D_FF = 2816
CONV_WIDTH = 3

IN_SPLITS = (ATT_Q_W, ATT_KV_W, ATT_KV_W, HG_K_W, HG_K_W, HG_V_W, HG_V_W, D_MODEL, D_MODEL)
IN_W = ATT_Q_W + 2 * ATT_KV_W + 2 * HG_K_W + 2 * HG_V_W + 2 * D_MODEL

kernel_name = "hybrid_swa_sink_hgrn2_gated_merge"


def rms_norm(x, gain):
    xf = x.astype(jnp.float32)
    y = xf * lax.rsqrt(jnp.mean(xf * xf, axis=-1, keepdims=True) + EPS)
    return (y * gain.astype(jnp.float32)).astype(x.dtype)


def alibi_slopes(n_heads):
    return 2.0 ** (-8.0 * jnp.arange(1, n_heads + 1, dtype=jnp.float32) / n_heads)


def swa_sink_attention(q, k, v, sinks):
    B, T = q.shape[0], q.shape[1]
    N = T // CHUNK
    KW = (LOOKBACK + 1) * CHUNK
    G, R, Dh = ATT_KV_HEADS, ATT_GROUP, ATT_HEAD_DIM
    qb = q.reshape(B, N, CHUNK, G, R, Dh)
    pad = ((0, 0), (LOOKBACK * CHUNK, 0), (0, 0))
    kp = jnp.pad(k, pad).reshape(B, N + LOOKBACK, CHUNK, G, Dh)
    vp = jnp.pad(v, pad).reshape(B, N + LOOKBACK, CHUNK, G, Dh)
    kb = jnp.concatenate([kp[:, j:j + N] for j in range(LOOKBACK + 1)], axis=2)
    vb = jnp.concatenate([vp[:, j:j + N] for j in range(LOOKBACK + 1)], axis=2)
    s = jnp.einsum('bncgrd,bnkgd->bngrck', qb, kb).astype(jnp.float32) * (Dh ** -0.5)
    dist = (jnp.arange(CHUNK)[:, None] + LOOKBACK * CHUNK) - jnp.arange(KW)[None, :]
    bias = -alibi_slopes(ATT_HEADS).reshape(G, R)[:, :, None, None] * jnp.abs(dist).astype(jnp.float32)
    k_pos = (jnp.arange(N)[:, None] - LOOKBACK) * CHUNK + jnp.arange(KW)[None, :]
    valid = (k_pos >= 0)[:, None, None, None, :]
    s = jnp.where(valid, s + bias, -jnp.inf)
    sink = sinks.astype(jnp.float32).reshape(G, R)[:, :, None, None]
    m = jnp.maximum(jnp.max(s, axis=-1, keepdims=True), sink)
    p = jnp.exp(s - m)
    probs = p / (jnp.sum(p, axis=-1, keepdims=True) + jnp.exp(sink - m))
    o = jnp.einsum('bngrck,bnkgd->bncgrd', probs.astype(v.dtype), vb)
    return o.reshape(B, T, ATT_Q_W)


def hgrn2_chunkwise(q, f_logit, i, lb):
    B, T = q.shape[0], q.shape[1]
    N = T // CHUNK
    f32 = jnp.float32
    lb = lb.astype(f32)
    f = lb + (1.0 - lb) * jax.nn.sigmoid(f_logit.astype(f32))
    shp_k = (B, N, CHUNK, HG_HEADS, HG_KEY_DIM)
    qc = q.astype(f32).reshape(shp_k) * (HG_KEY_DIM ** -0.5)
    kc = (1.0 - f).reshape(shp_k)
    vc = i.astype(f32).reshape(B, N, CHUNK, HG_HEADS, HG_VAL_DIM)
    b = jnp.cumsum(jnp.log(f).reshape(shp_k), axis=2)
    b_last = b[:, :, -1:]
    q_dec = qc * jnp.exp(b)
    k_inv = kc * jnp.exp(-b)
    k_end = kc * jnp.exp(b_last - b)
    causal = jnp.tril(jnp.ones((CHUNK, CHUNK), dtype=bool))
    a = jnp.where(causal, jnp.einsum('bnchd,bnshd->bnhcs', q_dec, k_inv), 0.0)
    o_intra = jnp.einsum('bnhcs,bnshv->bnchv', a, vc)
    ds = jnp.einsum('bnshd,bnshv->bnhdv', k_end, vc)
    decay = jnp.exp(b_last[:, :, 0])

    def step(state, inp):
        dec, d = inp
        return dec[..., None] * state + d, state

    s0 = jnp.zeros((B, HG_HEADS, HG_KEY_DIM, HG_VAL_DIM), f32)
    _, s_in = lax.scan(step, s0, (jnp.moveaxis(decay, 1, 0), jnp.moveaxis(ds, 1, 0)))
    s_in = jnp.moveaxis(s_in, 0, 1)
    o_inter = jnp.einsum('bnchd,bnhdv->bnchv', q_dec, s_in)
    return (o_intra + o_inter).reshape(B, T, HG_HEADS, HG_VAL_DIM)


def gated_head_rms_norm(o, gate, gain):
    B, T = o.shape[0], o.shape[1]
    y = o * lax.rsqrt(jnp.mean(o * o, axis=-1, keepdims=True) + EPS) * gain.astype(jnp.float32)
    y = y.reshape(B, T, HG_V_W) * jax.nn.silu(gate.astype(jnp.float32))
    return y.astype(gate.dtype)


def hybrid_mixer(h, g_mix, w_in, lb, sinks, g_onorm, w_branch_a, w_branch_b, w_mix_out):
    n = rms_norm(h, g_mix)
    proj = n @ w_in
    points = np.cumsum(IN_SPLITS)[:-1].tolist()
    aq, ak, av, hq, hf, hi, hg, gate_a, gate_b = jnp.split(proj, points, axis=-1)
    ya = swa_sink_attention(aq, ak, av, sinks) @ w_branch_a
    yb = gated_head_rms_norm(hgrn2_chunkwise(hq, hf, hi, lb), hg, g_onorm) @ w_branch_b
    y = jax.nn.sigmoid(gate_a) * ya + jax.nn.sigmoid(gate_b) * yb
    return y @ w_mix_out


def memory_cross_attention(h, mem, g_cross, g_mem, w_cq, w_ckv, w_co):
    B, T = h.shape[0], h.shape[1]
    M = mem.shape[1]
    nx = rms_norm(h, g_cross)
    nm = rms_norm(mem, g_mem)
    q = (nx @ w_cq).reshape(B, T, X_HEADS, X_HEAD_DIM)
    k, v = jnp.split(nm @ w_ckv, 2, axis=-1)
    k = k.reshape(B, M, X_HEADS, X_HEAD_DIM)
    v = v.reshape(B, M, X_HEADS, X_HEAD_DIM)
    s = jnp.einsum('bthd,bmhd->bhtm', q, k).astype(jnp.float32) * (X_HEAD_DIM ** -0.5)
    p = jax.nn.softmax(s, axis=-1)
    o = jnp.einsum('bhtm,bmhd->bthd', p.astype(v.dtype), v).reshape(B, T, D_MODEL)
    return o @ w_co


def conv_gated_ffn(h, g_ffn, w_ffn_in, conv_w, conv_b, w_ffn_down):
    n = rms_norm(h, g_ffn)
    u, gate = jnp.split(n @ w_ffn_in, 2, axis=-1)
    u = lax.conv_general_dilated(
        u, conv_w[:, None, :].astype(u.dtype), window_strides=(1,),
        padding=[(CONV_WIDTH - 1, 0)], dimension_numbers=('NWC', 'WIO', 'NWC'),
        feature_group_count=D_FF) + conv_b
    return (jax.nn.silu(u) * gate) @ w_ffn_down


def setup_inputs(seed: int = 0) -> dict:
    key = jax.random.key(seed)
    ks = jax.random.split(key, 24)
    f32 = jnp.float32

    def w(k, shape, fan_in):
        return jax.random.normal(k, shape, f32) * (fan_in ** -0.5)

    def gain(k, shape):
        return 1.0 + 0.05 * jax.random.normal(k, shape, f32)

    L = DEPTH
    return {
        'x': jax.random.normal(ks[0], (BATCH, SEQ, D_MODEL), f32),
        'mem': jax.random.normal(ks[1], (BATCH, N_MEM, D_MODEL), f32),
        'g_mix': gain(ks[2], (L, D_MODEL)),
        'w_in': w(ks[3], (L, D_MODEL, IN_W), D_MODEL),
        'lower_bounds': 1.0 + 0.1 * jax.random.normal(ks[4], (DEPTH + 1, HG_K_W), f32),
        'attn_sinks': 0.5 * jax.random.normal(ks[5], (L, ATT_HEADS), f32),
        'g_onorm': gain(ks[6], (L, HG_VAL_DIM)),
        'w_branch_a': w(ks[7], (L, ATT_Q_W, D_MODEL), ATT_Q_W),
        'w_branch_b': w(ks[8], (L, HG_V_W, D_MODEL), HG_V_W),
        'w_mix_out': w(ks[9], (L, D_MODEL, D_MODEL), D_MODEL),
        'g_cross': gain(ks[10], (L, D_MODEL)),
        'g_mem': gain(ks[11], (L, D_MODEL)),
        'w_cq': w(ks[12], (L, D_MODEL, D_MODEL), D_MODEL),
        'w_ckv': w(ks[13], (L, D_MODEL, 2 * D_MODEL), D_MODEL),
        'w_co': w(ks[14], (L, D_MODEL, D_MODEL), D_MODEL),
        'g_ffn': gain(ks[15], (L, D_MODEL)),
        'w_ffn_in': w(ks[16], (L, D_MODEL, 2 * D_FF), D_MODEL),
        'conv_w': w(ks[17], (L, CONV_WIDTH, D_FF), CONV_WIDTH),
        'conv_b': 0.02 * jax.random.normal(ks[18], (L, D_FF), f32),
        'w_ffn_down': w(ks[19], (L, D_FF, D_MODEL), D_FF),
        'g_final': gain(ks[20], (D_MODEL,)),
    }


def reference(x, mem, g_mix, w_in, lower_bounds, attn_sinks, g_onorm, w_branch_a, w_branch_b,
              w_mix_out, g_cross, g_mem, w_cq, w_ckv, w_co, g_ffn, w_ffn_in, conv_w, conv_b,
              w_ffn_down, g_final):
    lb_all = jnp.cumsum(jax.nn.softmax(lower_bounds.astype(jnp.float32), axis=0), axis=0)
    h = x
    for l in range(DEPTH):
        h = h + hybrid_mixer(h, g_mix[l], w_in[l], lb_all[l], attn_sinks[l], g_onorm[l],
                             w_branch_a[l], w_branch_b[l], w_mix_out[l])
        h = h + memory_cross_attention(h, mem, g_cross[l], g_mem[l], w_cq[l], w_ckv[l], w_co[l])
        h = h + conv_gated_ffn(h, g_ffn[l], w_ffn_in[l], conv_w[l], conv_b[l], w_ffn_down[l])
    return rms_norm(h, g_final)
```

```python
import contextlib
import numpy as np
import ml_dtypes
import concourse.bass as bass
import concourse.mybir as mybir
from concourse.bass_utils import run_bass_kernel_spmd

F32 = mybir.dt.float32
BF16 = mybir.dt.bfloat16
AF = mybir.ActivationFunctionType
ALU = mybir.AluOpType

T = 2048
D = 1024
NT = 16
NB = 4
EPS = 1e-6
DFF = 2816
NJ = 22
ENG = ("pe", "act", "dve", "pool", "sp")


class Op:
    __slots__ = ("eng", "fn", "deps", "inc", "idx", "dma", "slot", "dval")

    def __init__(self, eng, fn):
        self.eng = eng
        self.fn = fn
        self.deps = {}
        self.inc = False
        self.idx = 0
        self.dma = False
        self.slot = None
        self.dval = 0


class Sched:
    def __init__(self):
        self.ops = {e: [] for e in ENG}
        self.reg = {}
        self.slots = {}
        self.bar = {e: [] for e in ENG}
        self.dma_since = []
        self.nbank = 0

    def bank(self):
        b = self.nbank % 8
        self.nbank += 1
        return b

    def _dep(self, op, p, raw):
        if p is None or p is op:
            return
        cur = op.deps.get(id(p))
        if cur is None or (raw and not cur[1]):
            op.deps[id(p)] = (p, raw)

    def add(self, eng, fn, reads=(), writes=(), dma_slot=None):
        op = Op(eng, fn)
        if dma_slot is not None:
            op.dma = True
            op.slot = dma_slot
            self.slots[dma_slot] = self.slots.get(dma_slot, 0) + 16
            op.dval = self.slots[dma_slot]
            self.dma_since.append(op)
        for p in self.bar[eng]:
            self._dep(op, p, True)
        self.bar[eng] = []
        for k in reads:
            st = self.reg.get(k)
            if st is not None:
                self._dep(op, st[0], True)
        for k in writes:
            st = self.reg.get(k)
            if st is not None:
                self._dep(op, st[0], True)
                for r in st[1].values():
                    self._dep(op, r, False)
                for r in st[2]:
                    self._dep(op, r, True)
        for k in reads:
            st = self.reg.setdefault(k, [None, {}, []])
            if op.dma:
                st[2].append(op)
            else:
                st[1][eng] = op
        for k in writes:
            self.reg[k] = [op, {}, []]
        for (p, raw) in op.deps.values():
            if not p.dma:
                p.inc = True
        self.ops[eng].append(op)
        return op

    def barrier(self):
        last = [self.ops[e][-1] for e in ENG if self.ops[e]]
        pend = last + self.dma_since
        self.dma_since = []
        for e in ENG:
            self.bar[e] = list(pend)
        for p in pend:
            if not p.dma:
                p.inc = True

    def emit(self, nc, stack):
        for e in ENG:
            c = 0
            for op in self.ops[e]:
                if op.inc and not op.dma:
                    c += 1
                    op.idx = c
        esem = {e: stack.enter_context(nc.semaphore("sem_" + e)) for e in ENG}
        dsem = {s: stack.enter_context(nc.semaphore("dsem_" + s)) for s in self.slots}
        ops = self.ops
        slots = self.slots

        def run(engname, eng):
            waited = {}
            for op in ops[engname]:
                for (p, raw) in op.deps.values():
                    if p.dma:
                        key, sem, val = "d" + p.slot, dsem[p.slot], p.dval
                    else:
                        if p.eng == engname and engname == "pe":
                            continue
                        key, sem, val = p.eng, esem[p.eng], p.idx
                    if waited.get(key, 0) >= val:
                        continue
                    eng.wait_ge(sem, val)
                    waited[key] = val
                ins = op.fn(eng)
                if op.dma:
                    ins.then_inc(dsem[op.slot], 16)
                elif op.inc:
                    ins.then_inc(esem[engname], 1)
            if engname == "sp":
                for s, tot in slots.items():
                    eng.wait_ge(dsem[s], tot)

        with nc.Block() as block:
            @block.tensor
            def _(e):
                run("pe", e)

            @block.scalar
            def _(e):
                run("act", e)

            @block.vector
            def _(e):
                run("dve", e)

            @block.gpsimd
            def _(e):
                run("pool", e)

            @block.sync
            def _(e):
                run("sp", e)


def _consts():
    kk = np.arange(128)[:, None]
    cc = np.arange(128)[None, :]
    bias = np.zeros((2, 2, 128, 4, 128), np.float32)
    for g in range(2):
        for r in range(4):
            h = 4 * g + r
            slope = 2.0 ** (-(h + 1))
            dist = 128 + cc - kk
            b = -8.0 * slope * dist
            invalid = (cc >= 64) & (kk < 64)
            bias[g, 0, :, r, :] = np.where(invalid, -30000.0, b)
            dist = np.abs(cc - kk)
            b = -8.0 * slope * dist
            invalid = (cc < 64) & (kk >= 64)
            bias[g, 1, :, r, :] = np.where(invalid, -30000.0, b)
    bias = bias.reshape(4, 128, 512).transpose(1, 0, 2).reshape(128, 2048)
    maskT = ((kk // 64 == cc // 64) & (cc >= kk)).astype(np.float32)
    ident = np.eye(128, dtype=np.float32)
    cst = np.concatenate([bias, maskT, ident], axis=1).astype(ml_dtypes.bfloat16)
    return np.ascontiguousarray(cst)


class Mem:
    def __init__(self, arena, segs):
        self.arena = arena
        self.segs = [list(s) for s in segs]

    def alloc(self, shape, dtype, parts=128):
        n = 1
        for s in shape:
            n *= s
        nbytes = n * (2 if dtype == BF16 else 4)
        nbytes = (nbytes + 31) // 32 * 32
        for s in self.segs:
            if s[1] - s[0] >= nbytes:
                off = s[0]
                s[0] += nbytes
                break
        else:
            raise RuntimeError("SBUF stage alloc overflow: need %d, segs %s" % (nbytes, self.segs))
        v = self.arena[:, off // 4:(off + nbytes) // 4]
        if dtype == BF16:
            v = v.bitcast(BF16)
        v = v[:, 0:n]
        if len(shape) == 2:
            v = v.rearrange("p (a b) -> p a b", a=shape[0])
        elif len(shape) == 3:
            v = v.rearrange("p (a b c) -> p a b c", a=shape[0], b=shape[1])
        if parts != 128:
            v = v[0:parts]
        return v


def build(debug=None):
    nc = bass.Bass("TRN2", target_bir_lowering=False)
    S = Sched()
    stack = contextlib.ExitStack()

    def din(name, shape, dt=F32):
        return nc.dram_tensor(name, list(shape), dt, kind="ExternalInput").ap()

    x_d = din("x", [T, D])
    mem_d = din("mem", [256, D])
    g_mix_d = din("g_mix", [1, D])
    w_in_d = din("w_in", [1, D, 4864])
    lb_d = din("lower_bounds", [2, 512])
    sinks_d = din("attn_sinks", [1, 8])
    g_on_d = din("g_onorm", [1, 128])
    w_a_d = din("w_branch_a", [1, 512, D])
    w_b_d = din("w_branch_b", [1, 512, D])
    w_mo_d = din("w_mix_out", [1, D, D])
    g_cross_d = din("g_cross", [1, D])
    g_mem_d = din("g_mem", [1, D])
    w_cq_d = din("w_cq", [1, D, D])
    w_ckv_d = din("w_ckv", [1, D, 2 * D])
    w_co_d = din("w_co", [1, D, D])
    g_ffn_d = din("g_ffn", [1, D])
    w_fi_d = din("w_ffn_in", [1, D, 2 * DFF])
    cw_d = din("conv_w", [1, 3, DFF])
    cb_d = din("conv_b", [1, DFF])
    w_fd_d = din("w_ffn_down", [1, DFF, D])
    g_fin_d = din("g_final", [D])
    cst_d = din("cst", [128, 2304], BF16)
    out_d = nc.dram_tensor("out", [T, D], F32, kind="ExternalOutput").ap()
    dbg = {}

    NA = 204800 // 4
    arena = stack.enter_context(nc.sbuf_tensor("arena", [128, NA], F32))[:, :]
    banks = [stack.enter_context(nc.psum_tensor("ps%d" % i, [128, 512], F32))[:, :] for i in range(8)]

    P_END = 16384
    H0, H1 = P_END, P_END + 65536
    N0, N1 = H1, H1 + 32768
    F0, F1 = N1, 204800
    pm = Mem(arena, [(0, P_END)])
    cst = pm.alloc([2304], BF16)
    ident = cst[:, 2176:2304]
    maskT = cst[:, 2048:2176]
    ones = pm.alloc([128], BF16)
    cvec = pm.alloc([16], F32)
    stat = pm.alloc([3, 16], F32)
    small = pm.alloc([64], F32)
    gb = [pm.alloc([D], F32), pm.alloc([D], F32)]
    cwT = pm.alloc([NJ, 3], F32)
    cbT = pm.alloc([NJ], F32)
    hm = Mem(arena, [(H0, H1)])
    h = hm.alloc([NT, D], F32)
    nm_ = Mem(arena, [(N0, N1)])
    nT = nm_.alloc([8, T], BF16)

    eps_ap = cvec[:, 0:1]
    one_ap = cvec[:, 1:2]
    ss, ms, rstd = stat[:, 0, :], stat[:, 1, :], stat[:, 2, :]
    l0T, l1T = small[:, 0:4], small[:, 4:8]
    omlb, nomlb = small[:, 8:12], small[:, 12:16]
    gon = small[:, 16:17]
    sinkexp = small[0:64, 24:32]

    misc_n = [0]

    def dma(q, slot, out, in_, reads, writes, **kw):
        if slot == "misc":
            slot = "m%d" % misc_n[0]
            misc_n[0] += 1
        S.add(q, lambda e: e.dma_start(out=out, in_=in_, **kw), reads, writes, dma_slot=slot)

    def act(out, in_, func, reads, writes, bias=None, scale=None, accum_out=None):
        kw = {}
        if bias is not None:
            kw["bias"] = bias
        if scale is not None:
            kw["scale"] = scale
        if accum_out is not None:
            kw["accum_out"] = accum_out
        S.add("act", lambda e: e.activation(out=out, in_=in_, func=func, **kw), reads, writes)

    def tt(eng, out, in0, in1, op, reads, writes):
        S.add(eng, lambda e: e.tensor_tensor(out=out, in0=in0, in1=in1, op=op), reads, writes)

    def ts(eng, out, in0, s1, s2, op0, op1, reads, writes):
        S.add(eng, lambda e: e.tensor_scalar(out=out, in0=in0, scalar1=s1, scalar2=s2, op0=op0, op1=op1),
              reads, writes)

    def stt(out, in0, scalar, in1, op0, op1, reads, writes):
        S.add("dve", lambda e: e.scalar_tensor_tensor(out=out, in0=in0, scalar=scalar, in1=in1, op0=op0, op1=op1),
              reads, writes)

    def cp(eng, out, in_, reads, writes):
        if eng == "act":
            S.add("act", lambda e: e.activation(out=out, in_=in_, func=AF.Copy), reads, writes)
        else:
            S.add(eng, lambda e: e.tensor_copy(out=out, in_=in_), reads, writes)

    def mmg(out, pairs, reads, writes, first=True, last=True, skip=False):
        def fn(e):
            ins = None
            n = len(pairs)
            for i, (l, r) in enumerate(pairs):
                kw = {}
                if skip:
                    kw["skip_group_check"] = True
                ins = e.matmul(out, l, r, start=(first and i == 0), stop=(last and i == n - 1), **kw)
            return ins
        S.add("pe", fn, reads, writes)

    def ps_key(b):
        return ("ps", b)

    def nT_keys(blk):
        return [("nT", 4 * blk + i) for i in range(4)]

    def wload(dst, src, key, after=()):
        dma("pool", "w_" + key, dst, src, list(after), [("w", key)])

    dma("sp", "cst", cst, cst_d, [], [("cst",)])
    S.add("pool", lambda e: e.memset(ones, 1.0), [], [("ones",)])
    S.add("pool", lambda e: e.memset(cvec[:, 0:1], EPS), [], [("cvec",)])
    S.add("pool", lambda e: e.memset(cvec[:, 1:2], 1.0), [("cvec",)], [("cvec",)])

    def load_gb(slot, src2d):
        dma("sp", "gb%d" % slot, gb[slot], src2d.partition_broadcast(128), [], [("gb", slot)])

    def norm_stage(*a, **kw):
        for _ in norm_gen(*a, **kw):
            pass

    def norm_stage(*a, **kw):
        for _ in norm_gen(*a, **kw):
            pass

    def advance(gen, n):
        for _ in range(n):
            try:
                next(gen)
            except StopIteration:
                break

    def norm_gen(n_tiles, gslot, dstT, dst_key, src_dram=None, xt=None, xn=None, junk=None, xtag="xt",
                 xnk="xn", jk="junk"):
        G = 4 if n_tiles >= 4 else n_tiles
        ngrp = n_tiles // G

        def squares(g):
            for i in range(g * G, (g + 1) * G):
                if src_dram is not None:
                    src = xt[i % len(xt)]
                    sk = [(xtag, i % len(xt))]
                    dma("sp", "%s%d" % (xtag, i % len(xt)), src, src_dram[i * 128:(i + 1) * 128, :], [], sk)
                else:
                    src = h[:, i, :]
                    sk = [("h", i, 0), ("h", i, 1)]
                act(junk, src, AF.Square, sk, [(jk,), ("ss", i)], accum_out=ss[:, i:i + 1])
            gs = slice(g * G, (g + 1) * G)
            gk = list(range(g * G, (g + 1) * G))
            act(ms[:, gs], ss[:, gs], AF.Ln, [("ss", i) for i in gk] + [("cvec",)], [("ms", i) for i in gk],
                scale=1.0 / D, bias=eps_ap)
            act(rstd[:, gs], ms[:, gs], AF.Exp, [("ms", i) for i in gk], [("rstd", i) for i in gk], scale=-0.5)

        if src_dram is None:
            squares(0)
        for g in range(ngrp):
            if src_dram is not None:
                squares(g)
            pend = None
            for i in range(g * G, (g + 1) * G):
                if src_dram is not None:
                    src = xt[i % len(xt)]
                    sk = [(xtag, i % len(xt))]
                else:
                    src = h[:, i, :]
                    sk = [("h", i, 0), ("h", i, 1)]
                xnb = xn[i % 2]
                stt(xnb, src, rstd[:, i:i + 1], gb[gslot], ALU.mult, ALU.mult,
                    sk + [("rstd", i), ("gb", gslot)], [(xnk, i % 2)])
                b = S.bank()
                pb = banks[b].bitcast(BF16)

                def tr(e, xnb=xnb, pb=pb):
                    ins = None
                    for c in range(8):
                        ins = e.transpose(pb[:, c * 128:(c + 1) * 128], xnb[:, c * 128:(c + 1) * 128], ident)
                    return ins
                S.add("pe", tr, [(xnk, i % 2), ("cst",)], [ps_key(b)])
                if i == g * G and src_dram is None and g + 1 < ngrp:
                    squares(g + 1)
                if pend is not None:
                    pend()

                def mk(b=b, pb=pb, i=i):
                    def f():
                        cp("act", dstT[:, :, i * 128:(i + 1) * 128], pb.rearrange("p (c n) -> p c n", c=8),
                           [ps_key(b)], [(dst_key, i)])
                    return f
                pend = mk()
            pend()
            yield

    fm = Mem(arena, [(F0, F1), (H0, H1)])
    yT = fm.alloc([8, T], BF16)
    F_REST = fm.segs[0][0]
    xt = [fm.alloc([D], F32) for _ in range(4)]
    whA_pre = arena[:, F_REST // 4:(F_REST + 16384) // 4].bitcast(BF16).rearrange("p (c n) -> p c n", c=8)
    wqkv = fm.alloc([8, 768], BF16)
    wa = fm.alloc([8, D], BF16, parts=64)
    wga = fm.alloc([8, D], BF16)
    w_in = w_in_d[0].rearrange("(c p) n -> p c n", p=128)
    wload(wqkv, w_in[:, :, 0:768], "qkv")
    xn = [fm.alloc([D], BF16), fm.alloc([D], BF16)]
    junk = fm.alloc([D], BF16)
    load_gb(0, g_mix_d)
    norm_stage(NT, 0, nT, "nT", src_dram=x_d, xt=xt, xn=xn, junk=junk)
    wload(wa, w_a_d[0].rearrange("(h d) n -> d h n", d=64), "wa", after=[("nT", 7)])
    wload(wga, w_in[:, :, 2816:3840], "wga", after=[("nT", 7)])

    def dump(name, ap, keys, shape, dt):
        d = nc.dram_tensor("dbg_" + name, list(shape), dt, kind="ExternalOutput").ap()
        if len(ap.shape) == 3:
            dv = d.rearrange("p (c n) -> p c n", c=ap.shape[1])
        else:
            dv = d
        dma("sp", "dbg", dv, ap, keys, [])

    def finish():
        S.emit(nc, stack)
        return nc, stack

    if debug == "nT":
        dump("nT", nT, [("nT", i) for i in range(NT)], [128, 8 * T], BF16)
        return finish()

    qT = [fm.alloc([8, 512], BF16), fm.alloc([8, 512], BF16)]
    kT = fm.alloc([2, T], BF16)
    vv = fm.alloc([NT, 128], BF16)
    oT = fm.alloc([8, 512], BF16, parts=64)
    pT = [fm.alloc([2, 512], BF16), fm.alloc([2, 512], BF16)]
    den = [fm.alloc([512], F32, parts=64), fm.alloc([512], F32, parts=64)]
    sg = [fm.alloc([512], F32), fm.alloc([512], F32)]
    sink_b = fm.alloc([8, 128], F32, parts=64)

    dma("sp", "misc", sinkexp, sinks_d.partition_broadcast(64), [], [("sinkexp",)])
    act(sinkexp, sinkexp, AF.Exp, [("sinkexp",)], [("sinkexp",)])
    cp("dve", sink_b, sinkexp.unsqueeze(2).to_broadcast([64, 8, 128]), [("sinkexp",)], [("sink_b",)])
    for i in range(2):
        S.add("pool", lambda e, i=i: e.memset(qT[i][64:128], 0.0), [], [("qTz", i)])
    S.add("pool", lambda e: e.memset(kT[64:128], 0.0), [], [("kTz",)])

    def biasT(g, kind):
        o = (g * 2 + kind) * 512
        return cst[:, o:o + 512]

    def qk_proj(blk_, hh, eng):
        cs_ = slice(blk_ * 512, (blk_ + 1) * 512)
        b = S.bank()
        col = hh * 64 if hh < 8 else 512 + (hh - 8) * 64
        mmg(banks[b][0:64, :], [(wqkv[:, k, col:col + 64], nT[:, k, cs_]) for k in range(8)],
            nT_keys(blk_) + [("w", "qkv")], [ps_key(b)])
        if hh < 8:
            cp(eng, qT[blk_ % 2][0:64, hh, :], banks[b][0:64, :], [ps_key(b)], [("qT", blk_ % 2, hh)])
        else:
            cp(eng, kT[0:64, hh - 8, cs_], banks[b][0:64, :], [ps_key(b)], [("kT", hh - 8, blk_)])

    QK_SPLIT = [[8, 9], [0], [1], [2, 3], [4], [5], [6], [7]]
    for blk in range(NB):
        cs = slice(blk * 512, (blk + 1) * 512)
        qb = qT[blk % 2]
        if blk == 0:
            for hh in range(10):
                qk_proj(0, hh, "act")
        def v_proj(tile):
            b = S.bank()
            mmg(banks[b][:, 0:128], [(nT[:, k, tile * 128:(tile + 1) * 128], wqkv[:, k, 640:768]) for k in range(8)],
                [("nT", tile), ("w", "qkv")], [ps_key(b)])
            cp("dve", vv[:, tile, :], banks[b][:, 0:128], [ps_key(b)], [("v", tile)])
        if blk == 0:
            for tl in range(4):
                v_proj(tl)
        if blk == 1:
            xk = [("xt", i) for i in range(4)]
            dma("pool", "w_whA0", whA_pre[:, :, 0:512], w_in[:, :, 768 + 1024:768 + 1536], [], [("w", "whA_hi")] + xk)
            dma("pool", "w_whA1", whA_pre[:, :, 512:1024], w_in[:, :, 768 + 512:768 + 1024], [("w", "whA_hi")],
                [("w", "whA_hf")])
        items = [(tl, g) for tl in range(4) for g in range(2)]

        def att_S(it_):
            tl, g = items[it_]
            pp = it_ % 2
            j = 4 * blk + tl
            kts = ([(j - 1, 0)] if j >= 1 else []) + [(j, 1)]
            rq = qb[:, 4 * g:4 * g + 4, tl * 128:(tl + 1) * 128]
            qkeys = [("qT", blk % 2, 4 * g + r) for r in range(4)] + [("qTz", blk % 2), ("kTz",)]
            for idx, (kt, kind) in enumerate(kts):
                b = S.bank()

                def fn(e, b=b, kt=kt, kind=kind, rq=rq, g=g):
                    e.matmul(banks[b].rearrange("p (a n) -> p a n", a=4), kT[:, g, kt * 128:(kt + 1) * 128], rq,
                             start=True, stop=False)
                    return e.matmul(banks[b], ident, biasT(g, kind), start=False, stop=True)
                S.add("pe", fn, qkeys + [("kT", g, kt // 4), ("cst",)], [ps_key(b)])
                act(pT[pp][:, idx, :], banks[b], AF.Exp, [ps_key(b)], [("pT", pp, idx)], scale=0.125)

        def att_PV(it_):
            tl, g = items[it_]
            pp = it_ % 2
            j = 4 * blk + tl
            kts = ([(j - 1, 0)] if j >= 1 else []) + [(j, 1)]
            pk = [("pT", pp, idx) for idx in range(len(kts))]
            bO = S.bank()
            mmg(banks[bO][0:64, :], [(vv[:, kt, g * 64:(g + 1) * 64], pT[pp][:, idx, :])
                                     for idx, (kt, kind) in enumerate(kts)],
                pk + [("v", kt) for kt, _ in kts], [ps_key(bO)])
            bD = S.bank()
            mmg(banks[bD][0:64, :], [(ones[:, 0:64], pT[pp][:, idx, :]) for idx in range(len(kts))],
                pk + [("ones",)], [ps_key(bD)])
            def lnf(e, pp=pp, bD=bD, g=g):
                ins = None
                for r in range(4):
                    ins = e.activation(out=den[pp][:, r * 128:(r + 1) * 128], in_=banks[bD][0:64, r * 128:(r + 1) * 128],
                                       func=AF.Ln, bias=sinkexp[:, 4 * g + r:4 * g + r + 1])
                return ins
            S.add("act", lnf, [ps_key(bD), ("sinkexp",)], [("den", pp)])
            act(den[pp], den[pp], AF.Exp, [("den", pp)], [("den", pp)], scale=-1.0)
            tt("dve", oT[:, 4 * g:4 * g + 4, tl * 128:(tl + 1) * 128],
               banks[bO][0:64, :].rearrange("p (a n) -> p a n", a=4),
               den[pp].rearrange("p (a n) -> p a n", a=4), ALU.mult,
               [ps_key(bO), ("den", pp)], [("oT", g, tl)])

        att_S(0)
        for it_ in range(len(items)):
            if it_ + 1 < len(items):
                att_S(it_ + 1)
            att_PV(it_)
            if blk + 1 < NB:
                for hh in QK_SPLIT[it_]:
                    qk_proj(blk + 1, hh, "dve")
                if it_ % 2 == 1:
                    v_proj(4 * (blk + 1) + it_ // 2)
        okeys = [("oT", g, tl) for g in range(2) for tl in range(4)]
        for fc in range(8):
            fs = slice(fc * 128, (fc + 1) * 128)
            bY = S.bank()
            mmg(banks[bY], [(wa[:, hh, fs], oT[:, hh, :]) for hh in range(8)], okeys + [("w", "wa")], [ps_key(bY)])
            bG = S.bank()
            mmg(banks[bG], [(wga[:, k, fs], nT[:, k, cs]) for k in range(8)],
                nT_keys(blk) + [("w", "wga")], [ps_key(bG)])
            act(sg[fc % 2], banks[bG], AF.Sigmoid, [ps_key(bG)], [("sg", fc % 2)])
            tt("dve", yT[:, fc, cs], banks[bY], sg[fc % 2], ALU.mult, [ps_key(bY), ("sg", fc % 2)], [("yT", fc, blk)])

    yT_keys = [("yT", fc, blk) for fc in range(8) for blk in range(NB)]
    if debug == "yA":
        dump("yT", yT, yT_keys, [128, 8 * T], BF16)
        return finish()

    S.barrier()
    fm = Mem(arena, [(F_REST, F1), (H0, H1)])
    whA = fm.alloc([8, 1024], BF16)
    whB = fm.alloc([8, 1024], BF16)
    wb = fm.alloc([4, D], BF16)
    wgb = fm.alloc([8, D], BF16)
    resetm = pm.alloc([512], F32)
    Scar = fm.alloc([4, 128], F32)
    vc = fm.alloc([4, 512], BF16)
    sgm = fm.alloc([4, 512], F32)
    slu = fm.alloc([4, 512], BF16)
    uT = fm.alloc([4, 512], BF16)
    sg = [fm.alloc([512], F32), fm.alloc([512], F32)]
    sgg = sg
    tmpy = [fm.alloc([512], F32), fm.alloc([512], F32)]
    HB = []
    for i in range(2):
        HB.append(dict(
            A=fm.alloc([512], F32), B=fm.alloc([512], F32), C=fm.alloc([512], F32),
            q_dec=fm.alloc([512], BF16), k_inv=fm.alloc([512], BF16), k_endT=fm.alloc([512], BF16),
            k_end=fm.alloc([4, 128], BF16), aT=fm.alloc([4, 128], BF16),
            Sfp=fm.alloc([7, 128], F32), Sb=fm.alloc([8, 128], BF16), decay=fm.alloc([8, 1], F32)))

    dma("pool", "w_whB0", whB[:, :, 0:512], w_in[:, :, 768 + 1536:768 + 2048], [], [("w", "whB_hg")])
    dma("pool", "w_whB1", whB[:, :, 512:1024], w_in[:, :, 768:768 + 512], [("w", "whB_hg")], [("w", "whB_hq")])
    wload(wgb, w_in[:, :, 3840:4864], "wgb", after=[("w", "whB_hq")])
    wload(wb, w_b_d[0].rearrange("(h v) n -> v h n", v=128), "wb", after=[("w", "whB_hq")])
    dma("sp", "misc", l0T, lb_d[0].rearrange("(h p) -> p h", p=128), [], [("l0T",)], allow_slow_non_contiguous=True)
    dma("sp", "misc", l1T, lb_d[1].rearrange("(h p) -> p h", p=128), [], [("l1T",)], allow_slow_non_contiguous=True)
    dma("sp", "misc", gon, g_on_d[0].rearrange("(p o) -> p o", o=1), [], [("gon",)])
    tt("dve", omlb, l1T, l0T, ALU.subtract, [("l0T",), ("l1T",)], [("omlb",)])
    act(omlb, omlb, AF.Sigmoid, [("omlb",)], [("omlb",)])
    ts("dve", nomlb, omlb, -1.0, None, ALU.mult, ALU.bypass, [("omlb",)], [("nomlb",)])
    S.add("pool", lambda e: e.memset(resetm, 1.0), [], [("resetm",)])
    S.add("pool", lambda e: e.memset(resetm.rearrange("p (c s) -> p c s", s=64)[:, :, 0:1], 0.0),
          [("resetm",)], [("resetm",)])
    S.add("pool", lambda e: e.memset(Scar, 0.0), [], [("Scar", hh) for hh in range(4)])

    def hgrn_stages(blk, hh):
        cs = slice(blk * 512, (blk + 1) * 512)
        s = hh % 2
        X = HB[s]
        A, Bf, C = X["A"], X["B"], X["C"]
        q_dec, k_inv, k_endT, k_end, aT = X["q_dec"], X["k_inv"], X["k_endT"], X["k_end"], X["aT"]
        osq = k_endT
        Sfp, Sb, decay = X["Sfp"], X["Sb"], X["decay"]

        def K(n):
            return (n, s)
        act(A, sgm[:, hh, :], AF.Ln, [("sgm", hh), ("nomlb",), ("cvec",)], [K("A")],
            scale=nomlb[:, hh:hh + 1], bias=one_ap)
        bQ = S.bank()
        mmg(banks[bQ], [(whB[:, k, 512 + hh * 128:512 + (hh + 1) * 128], nT[:, k, cs]) for k in range(8)],
            nT_keys(blk) + [("w", "whB_hq")], [ps_key(bQ)])
        yield
        S.add("dve", lambda e: e.tensor_tensor_scan(out=A, data0=resetm, data1=A, initial=0.0,
                                                    op0=ALU.mult, op1=ALU.add),
              [("resetm",), K("A")], [K("A")])
        yield
        act(Bf, A, AF.Exp, [K("A")], [K("B")])
        act(C, A, AF.Exp, [K("A")], [K("C")], scale=-1.0)
        act(decay, A.rearrange("p (c s) -> p c s", s=64)[:, :, 63:64], AF.Exp, [K("A")], [K("decay")])
        yield
        stt(q_dec, banks[bQ], float(128 ** -0.5), Bf, ALU.mult, ALU.mult, [ps_key(bQ), K("B")], [K("q_dec")])
        stt(k_inv, sgm[:, hh, :], omlb[:, hh:hh + 1], C, ALU.mult, ALU.mult,
            [("sgm", hh), ("omlb",), K("C")], [K("k_inv")])
        yield
        tt("dve", k_endT.rearrange("p (c s) -> p c s", s=64), k_inv.rearrange("p (c s) -> p c s", s=64),
           decay.to_broadcast([128, 8, 64]), ALU.mult, [K("k_inv"), K("decay")], [K("k_endT")])
        yield
        bT_ = S.bank()
        pbT = banks[bT_].bitcast(BF16)

        def trk(e):
            ins = None
            for tl in range(4):
                ins = e.transpose(pbT[:, tl * 128:(tl + 1) * 128], k_endT[:, tl * 128:(tl + 1) * 128], ident)
            return ins
        S.add("pe", trk, [K("k_endT"), ("cst",)], [ps_key(bT_)])
        bA = S.bank()

        def af(e):
            ins = None
            for tl in range(4):
                sl = slice(tl * 128, (tl + 1) * 128)
                ins = e.matmul(banks[bA][:, sl], k_inv[:, sl], q_dec[:, sl], start=True, stop=True)
            return ins
        S.add("pe", af, [K("k_inv"), K("q_dec")], [ps_key(bA)])
        yield
        cp("act", k_end.rearrange("p a n -> p (a n)"), pbT[:, 0:512], [ps_key(bT_)], [K("k_end")])
        tt("dve", aT, banks[bA].rearrange("p (a n) -> p a n", a=4), maskT.unsqueeze(1).to_broadcast([128, 4, 128]),
           ALU.mult, [ps_key(bA), ("cst",)], [K("aT")])
        cp("pool", Sb[:, 0, :], Scar[:, hh, :], [("Scar", hh)], [K("Sb0")])
        yield
        bD = [S.bank(), S.bank()]

        def dsf(e):
            ins = None
            for n in range(8):
                tl, half = n // 2, n % 2
                rows = slice(half * 64, half * 64 + 64)
                ins = e.matmul(banks[bD[half]][:, tl * 128:(tl + 1) * 128], k_end[rows, tl, :],
                               vc[rows, tl, hh * 128:(hh + 1) * 128], start=True, stop=True)
            return ins
        S.add("pe", dsf, [K("k_end")] + [("vc", tl) for tl in range(4)], [ps_key(bD[0]), ps_key(bD[1])])
        yield
        for n in range(8):
            src = Scar[:, hh, :] if n == 0 else Sfp[:, n - 1, :]
            dst = Scar[:, hh, :] if n == 7 else Sfp[:, n, :]
            rk = [("Scar", hh)] if n == 0 else [K("Sfp%d" % (n - 1))]
            wk = [("Scar", hh)] if n == 7 else [K("Sfp%d" % n)]
            stt(dst, src, decay[:, n, :], banks[bD[n % 2]][:, (n // 2) * 128:(n // 2 + 1) * 128],
                ALU.mult, ALU.add, rk + [K("decay"), ps_key(bD[n % 2])], wk)
            if n == 3:
                yield
        cp("dve", Sb[:, 1:8, :], Sfp[:, 0:7, :], [K("Sfp%d" % n) for n in range(7)], [K("Sb1")])
        yield
        bO = S.bank()

        def of(e):
            ins = None
            for tl in range(4):
                sl = slice(tl * 128, (tl + 1) * 128)
                e.matmul(banks[bO][:, sl], vc[:, tl, hh * 128:(hh + 1) * 128], aT[:, tl, :], start=True, stop=False,
                         skip_group_check=True)
                for cc in range(2):
                    n = 2 * tl + cc
                    s2 = slice(tl * 128 + cc * 64, tl * 128 + cc * 64 + 64)
                    ins = e.matmul(banks[bO][:, s2], Sb[:, n, :], q_dec[:, s2], start=False, stop=(cc == 1),
                                   skip_group_check=True)
            return ins
        S.add("pe", of, [K("aT"), K("q_dec"), K("Sb0"), K("Sb1")] + [("vc", tl) for tl in range(4)], [ps_key(bO)])
        yield
        act(osq, banks[bO], AF.Square, [ps_key(bO)], [K("k_endT")])
        yield
        bN = S.bank()
        mmg(banks[bN], [(ones, osq)], [K("k_endT"), ("ones",)], [ps_key(bN)])
        yield
        act(Bf, banks[bN], AF.Ln, [ps_key(bN), ("cvec",)], [K("B")], scale=1.0 / 128, bias=eps_ap)
        act(Bf, Bf, AF.Exp, [K("B")], [K("B")], scale=-0.5)
        yield
        tt("dve", C, banks[bO], Bf, ALU.mult, [ps_key(bO), K("B")], [K("C")])
        tt("dve", uT[:, hh, :], C, slu[:, hh, :], ALU.mult, [K("C"), ("slu", hh)], [("uT", hh)])
        yield

    sgB = gb[0][:, :].bitcast(BF16)
    sgB2 = gb[1][:, :].bitcast(BF16)

    def sgB_ap(fc):
        src = sgB if fc < 4 else sgB2
        return src[:, (fc % 4) * 512:(fc % 4 + 1) * 512]

    def sig_from_psum(out, bank, neg, tmp, tkey, rkeys, wkeys):
        act(tmp, banks[bank], AF.Exp, [ps_key(bank)] + rkeys, [tkey], scale=(1.0 if neg else -1.0))
        act(tmp, tmp, AF.Ln, [tkey, ("cvec",)], [tkey], bias=one_ap)
        act(out, tmp, AF.Exp, [tkey], wkeys, scale=-1.0)

    act_next = []

    def p1_chunks(blk, hh):
        cs = slice(blk * 512, (blk + 1) * 512)

        def c_f():
            bF = S.bank()
            mmg(banks[bF], [(whA[:, k, 512 + hh * 128:512 + (hh + 1) * 128], nT[:, k, cs]) for k in range(8)],
                nT_keys(blk) + [("w", "whA_hf")], [ps_key(bF)])
            act_next.append(lambda: sig_from_psum(sgm[:, hh, :], bF, True, sg[0], ("sg", 0), [], [("sgm", hh)]))

        def c_g():
            bG = S.bank()
            mmg(banks[bG], [(whB[:, k, hh * 128:(hh + 1) * 128], nT[:, k, cs]) for k in range(8)],
                nT_keys(blk) + [("w", "whB_hg")], [ps_key(bG)])

            def a():
                sig_from_psum(sg[1], bG, False, sg[1], ("sg", 1), [], [("sg", 1)])
                deferred.append(lambda: stt(slu[:, hh, :], banks[bG], gon, sg[1], ALU.mult, ALU.mult,
                                            [ps_key(bG), ("sg", 1), ("gon",)], [("slu", hh)]))
            act_next.append(a)
        return [c_f, c_g]

    def gate_chunk(blk, fc):
        cs = slice(blk * 512, (blk + 1) * 512)

        def c():
            bG = S.bank()
            mmg(banks[bG], [(wgb[:, k, fc * 128:(fc + 1) * 128], nT[:, k, cs]) for k in range(8)],
                nT_keys(blk) + [("w", "wgb")], [ps_key(bG)])
            act_next.append(lambda: sig_from_psum(sgB_ap(fc), bG, False, sg[fc % 2], ("sg", fc % 2), [],
                                                  [("sgB", fc)]))
        return c

    deferred = []

    def flush_deferred():
        n = len(deferred)
        for _ in range(n):
            deferred.pop(0)()
        n = len(act_next)
        for _ in range(n):
            act_next.pop(0)()

    def run_pair(blk, pair, fillers):
        gens = [hgrn_stages(blk, 2 * pair), hgrn_stages(blk, 2 * pair + 1)]
        alive = True
        rnd = 0
        while alive:
            alive = False
            for g_ in gens:
                try:
                    next(g_)
                    alive = True
                except StopIteration:
                    pass
            rnd += 1
            flush_deferred()
            if fillers and rnd >= 2:
                fillers.pop(0)()
        while fillers:
            flush_deferred()
            fillers.pop(0)()
        flush_deferred()
        flush_deferred()

    nreg = Mem(arena, [(N0, N1)])
    wmo = nreg.alloc([8, D], BF16)
    wckK = nreg.alloc([8, D], BF16)
    w_mo = w_mo_d[0].rearrange("(c p) n -> p c n", p=128)

    def wmo_prefetch():
        dma("pool", "w_wmo", wmo, w_mo, [], [("w", "wmo")] + [("nT", i) for i in range(NT)])

    for blk in range(NB):
        cs = slice(blk * 512, (blk + 1) * 512)
        for tl in range(4):
            tile = 4 * blk + tl
            b = S.bank()
            mmg(banks[b], [(nT[:, k, tile * 128:(tile + 1) * 128], whA[:, k, 0:512]) for k in range(8)],
                [("nT", tile), ("w", "whA_hi")], [ps_key(b)])
            cp("dve", vc[:, tl, :], banks[b], [ps_key(b)], [("vc", tl)])
        if blk == 0:
            pc = [p1_chunks(0, hh) for hh in range(2)]
            for c in (pc[0][0], pc[1][0], pc[0][1], pc[1][1]):
                c()
                flush_deferred()
            flush_deferred()
            flush_deferred()
        f0 = [gate_chunk(blk, fc) for fc in range(4)]
        p1 = p1_chunks(blk, 2) + p1_chunks(blk, 3)
        f0 = [f0[0], p1[0], f0[1], p1[1], f0[2], p1[2], f0[3], p1[3]]
        run_pair(blk, 0, f0)
        f1 = [gate_chunk(blk, fc) for fc in range(4, 8)]
        if blk + 1 < NB:
            p1 = p1_chunks(blk + 1, 0) + p1_chunks(blk + 1, 1)
            f1 = [f1[0], p1[0], f1[1], p1[1], f1[2], p1[2], f1[3], p1[3]]
        if blk == NB - 1:
            f1 = f1 + [wmo_prefetch]
        run_pair(blk, 1, f1)
        for fc in range(8):
            fs = slice(fc * 128, (fc + 1) * 128)
            bY = S.bank()
            mmg(banks[bY], [(wb[:, hh, fs], uT[:, hh, :]) for hh in range(4)],
                [("uT", hh) for hh in range(4)] + [("w", "wb")], [ps_key(bY)])
            tt("dve", tmpy[fc % 2], banks[bY], sgB_ap(fc), ALU.mult, [ps_key(bY), ("sgB", fc)],
               [("tmpy", fc % 2)])
            tt("pool", yT[:, fc, cs], yT[:, fc, cs], tmpy[fc % 2], ALU.add, [("yT", fc, blk), ("tmpy", fc % 2)],
               [("yT", fc, blk)])

    if debug == "yB":
        dump("yT", yT, yT_keys, [128, 8 * T], BF16)
        return finish()

    S.barrier()
    fm = Mem(arena, [(F_REST, F1)])
    xt = [fm.alloc([D], F32), fm.alloc([D], F32)]
    wckV = fm.alloc([8, D], BF16)
    KV0 = fm.segs[0][0]
    KT = fm.alloc([8, 256], BF16)
    Vm = fm.alloc([2, D], BF16)
    KV1 = fm.segs[0][0]
    nmT = fm.alloc([8, 256], BF16)
    xtm = [fm.alloc([D], F32), fm.alloc([D], F32)]
    xnm = [fm.alloc([D], BF16), fm.alloc([D], BF16)]
    junkm = fm.alloc([D], BF16)
    load_gb(0, g_mem_d)
    w_ckv = w_ckv_d[0].rearrange("(c p) n -> p c n", p=128)
    wload(wckK, w_ckv[:, :, 0:D], "wckK")
    wload(wckV, w_ckv[:, :, D:2 * D], "wckV", after=[("w", "wckK")])

    def mem_kv():
        norm_stage(2, 0, nmT, "nmT", src_dram=mem_d, xt=xtm, xn=xnm, junk=junkm, xtag="xm", xnk="xnm", jk="junkm")
        nm_keys = [("nmT", 0), ("nmT", 1)]
        for c8 in range(8):
            b = S.bank()
            mmg(banks[b][:, 0:256], [(wckK[:, k, c8 * 128:(c8 + 1) * 128], nmT[:, k, :]) for k in range(8)],
                nm_keys + [("w", "wckK")], [ps_key(b)])
            cp("act", KT[:, c8, :], banks[b][:, 0:256], [ps_key(b)], [("KT", c8)])
        for mt in range(2):
            for half in range(2):
                b = S.bank()
                mmg(banks[b], [(nmT[:, k, mt * 128:(mt + 1) * 128], wckV[:, k, half * 512:(half + 1) * 512])
                               for k in range(8)], nm_keys + [("w", "wckV")], [ps_key(b)])
                cp("act", Vm[:, mt, half * 512:(half + 1) * 512], banks[b], [ps_key(b)], [("Vm", mt, half)])

    for tile in range(NT):
        tsl = slice(tile * 128, (tile + 1) * 128)
        if tile == 10 and debug != "hB":
            mem_kv()
        dma("sp", "xt%d" % (tile % 2), xt[tile % 2], x_d[tsl, :], [], [("xt", tile % 2)])
        for half in range(2):
            hs = slice(half * 512, (half + 1) * 512)
            b = S.bank()
            mmg(banks[b], [(yT[:, k, tsl], wmo[:, k, hs]) for k in range(8)],
                [("yT", k, tile // 4) for k in range(8)] + [("w", "wmo")], [ps_key(b)])
            tt("dve", h[:, tile, hs], banks[b], xt[tile % 2][:, hs], ALU.add, [ps_key(b), ("xt", tile % 2)],
               [("h", tile, half)])

    h_keys = [("h", i, hf) for i in range(NT) for hf in range(2)]
    if debug == "hB":
        dump("h", h, h_keys, [128, NT * D], F32)
        return finish()


    S.barrier()
    fm = Mem(arena, [(F0, KV0), (KV1, F1)])
    xn = [fm.alloc([D], BF16), fm.alloc([D], BF16)]
    junk = fm.alloc([D], BF16)
    load_gb(1, g_cross_d)
    wcq = fm.alloc([8, D], BF16)
    wco = fm.alloc([8, D], BF16)
    qTc = fm.alloc([8, 512], BF16)
    oTc = fm.alloc([8, 512], BF16)
    pTc = [fm.alloc([2, 512], BF16), fm.alloc([2, 512], BF16)]
    rec = [fm.alloc([512], F32), fm.alloc([512], F32)]
    wload(wcq, w_cq_d[0].rearrange("(c p) n -> p c n", p=128), "wcq")
    wload(wco, w_co_d[0].rearrange("(c p) n -> p c n", p=128), "wco", after=[("w", "wcq")])
    ngen = norm_gen(NT, 1, nT, "nT", xn=xn, junk=junk)
    for blk in range(NB):
        cs = slice(blk * 512, (blk + 1) * 512)
        advance(ngen, 2 if blk == 0 else 1)
        for c8 in range(8):
            b = S.bank()
            mmg(banks[b], [(wcq[:, k, c8 * 128:(c8 + 1) * 128], nT[:, k, cs]) for k in range(8)],
                nT_keys(blk) + [("w", "wcq")], [ps_key(b)])
            cp("act" if c8 % 2 else "dve", qTc[:, c8, :], banks[b], [ps_key(b)], [("qTc", c8)])
        def x_S(head):
            pp = head % 2
            for mt in range(2):
                b = S.bank()
                mmg(banks[b], [(KT[:, head * 2 + dc, mt * 128:(mt + 1) * 128], qTc[:, head * 2 + dc, :]) for dc in range(2)],
                    [("KT", head * 2), ("KT", head * 2 + 1), ("qTc", head * 2), ("qTc", head * 2 + 1)], [ps_key(b)])
                act(pTc[pp][:, mt, :], banks[b], AF.Exp, [ps_key(b)], [("pTc", pp, mt)], scale=1.0 / 16)

        def x_PV(head):
            pp = head % 2
            pk = [("pTc", pp, 0), ("pTc", pp, 1)]
            bD = S.bank()
            mmg(banks[bD], [(ones, pTc[pp][:, mt, :]) for mt in range(2)], pk + [("ones",)], [ps_key(bD)])
            act(rec[pp], banks[bD], AF.Ln, [ps_key(bD)], [("rec", pp)])
            act(rec[pp], rec[pp], AF.Exp, [("rec", pp)], [("rec", pp)], scale=-1.0)
            for dc in range(2):
                bO = S.bank()
                c0 = head * 256 + dc * 128
                mmg(banks[bO], [(Vm[:, mt, c0:c0 + 128], pTc[pp][:, mt, :]) for mt in range(2)],
                    pk + [("Vm", mt, c0 // 512) for mt in range(2)], [ps_key(bO)])
                tt("dve", oTc[:, head * 2 + dc, :], banks[bO], rec[pp], ALU.mult, [ps_key(bO), ("rec", pp)],
                   [("oTc", head * 2 + dc)])

        x_S(0)
        for head in range(4):
            if head + 1 < 4:
                x_S(head + 1)
            x_PV(head)
        for tl in range(4):
            tile = 4 * blk + tl
            for half in range(2):
                hs = slice(half * 512, (half + 1) * 512)
                b = S.bank()
                mmg(banks[b], [(oTc[:, k, tl * 128:(tl + 1) * 128], wco[:, k, hs]) for k in range(8)],
                    [("oTc", k) for k in range(8)] + [("w", "wco")], [ps_key(b)])
                tt("dve", h[:, tile, hs], banks[b], h[:, tile, hs], ALU.add, [ps_key(b), ("h", tile, half)],
                   [("h", tile, half)])

    if debug == "hC":
        dump("h", h, h_keys, [128, NT * D], F32)
        return finish()

    S.barrier()
    fm = Mem(arena, [(F0, F1)])
    wfi = [fm.alloc([8, 1024], BF16), fm.alloc([8, 1024], BF16)]
    wfd = [fm.alloc([4, D], BF16), fm.alloc([4, D], BF16)]
    actT = fm.alloc([4, T], BF16)
    ub = [fm.alloc([520], F32), fm.alloc([520], F32)]
    cbuf = [fm.alloc([512], F32), fm.alloc([512], F32)]
    sbf = [fm.alloc([512], F32), fm.alloc([512], F32)]
    xnbuf = fm.alloc([D], F32)
    xn = [xnbuf[:, 0:512].bitcast(BF16), xnbuf[:, 512:1024].bitcast(BF16)]
    junk = fm.alloc([D], BF16)
    load_gb(0, g_ffn_d)
    load_gb(1, g_fin_d.rearrange("(o n) -> o n", o=1))
    ot = [fm.alloc([D], F32), xnbuf]
    for w in range(3):
        dma("sp", "misc", cwT[:, :, w], cw_d[0, w].rearrange("(j p) -> p j", p=128), [], [("cwT", w)],
            allow_slow_non_contiguous=True)
    dma("sp", "misc", cbT, cb_d[0].rearrange("(j p) -> p j", p=128), [], [("cbT",)], allow_slow_non_contiguous=True)
    w_fi = w_fi_d[0].rearrange("(c p) n -> p c n", p=128)
    groups = [(0, 2), (2, 4), (6, 4), (10, 4), (14, 4), (18, 4)]

    def ffn_load(gi):
        j0, nj = groups[gi]
        ws = gi % 2
        wload(wfi[ws][:, :, 0:nj * 128], w_fi[:, :, j0 * 128:(j0 + nj) * 128], "wfiu%d" % ws)
        wload(wfi[ws][:, :, 512:512 + nj * 128], w_fi[:, :, DFF + j0 * 128:DFF + (j0 + nj) * 128], "wfig%d" % ws)
        wload(wfd[ws][:, 0:nj, :], w_fd_d[0][j0 * 128:(j0 + nj) * 128, :].rearrange("(j p) n -> p j n", p=128),
              "wfd%d" % ws, after=[("w", "wfiu%d" % ws), ("w", "wfig%d" % ws)])

    fin_pend = [None]

    def final_norm_tile(i):
        src = h[:, i, :]
        skeys = [("h", i, 0), ("h", i, 1)]
        act(junk, src, AF.Square, skeys, [("junk",), ("ss", i)], accum_out=ss[:, i:i + 1])
        act(ms[:, i:i + 1], ss[:, i:i + 1], AF.Ln, [("ss", i), ("cvec",)], [("ms", i)], scale=1.0 / D, bias=eps_ap)
        act(rstd[:, i:i + 1], ms[:, i:i + 1], AF.Exp, [("ms", i)], [("rstd", i)], scale=-0.5)
        if fin_pend[0] is not None:
            fin_pend[0]()

        def second():
            o2 = ot[i % 2]
            stt(o2[:, 0:512], src[:, 0:512], rstd[:, i:i + 1], gb[1][:, 0:512], ALU.mult, ALU.mult,
                skeys + [("rstd", i), ("gb", 1)], [("ot", i % 2, 0)])
            S.add("act", lambda e: e.activation(out=cbuf[i % 2], in_=src[:, 512:1024], func=AF.Copy,
                                                scale=rstd[:, i:i + 1]),
                  skeys + [("rstd", i)], [("cbuf", i % 2)])
            tt("pool", o2[:, 512:1024], cbuf[i % 2], gb[1][:, 512:1024], ALU.mult, [("cbuf", i % 2), ("gb", 1)],
               [("ot", i % 2, 1)])
            dma("sp", "out%d" % (i % 2), out_d[i * 128:(i + 1) * 128, :], o2,
                [("ot", i % 2, 0), ("ot", i % 2, 1)], [])
        fin_pend[0] = second
        if i == NT - 1:
            second()
            fin_pend[0] = None

    ffn_load(0)
    ngen = norm_gen(NT, 0, nT, "nT", xn=xn, junk=junk)
    cwk = [("cwT", w) for w in range(3)] + [("cbT",)]
    it = 0
    pendB = [None]
    for gi, (j0, nj) in enumerate(groups):
        ws = gi % 2
        if gi + 1 < len(groups):
            ffn_load(gi + 1)
        for jj in range(nj):
            j = j0 + jj
            for blk in range(NB):
                cs = slice(blk * 512, (blk + 1) * 512)
                cur, prv = it % 2, (it + 1) % 2
                if gi == 0 and jj == 0:
                    advance(ngen, 2 if blk == 0 else 1)
                bU = S.bank()
                mmg(banks[bU], [(wfi[ws][:, k, jj * 128:(jj + 1) * 128], nT[:, k, cs]) for k in range(8)],
                    nT_keys(blk) + [("w", "wfiu%d" % ws)], [ps_key(bU)])
                bG = S.bank()
                mmg(banks[bG], [(wfi[ws][:, k, 512 + jj * 128:512 + (jj + 1) * 128], nT[:, k, cs]) for k in range(8)],
                    nT_keys(blk) + [("w", "wfig%d" % ws)], [ps_key(bG)])
                if blk == 0:
                    S.add("pool", lambda e, cur=cur: e.memset(ub[cur][:, 0:2], 0.0), [], [("ubh", cur)])
                else:
                    cp("pool", ub[cur][:, 0:2], ub[prv][:, 512:514], [("ub", prv)], [("ubh", cur)])
                cp("act", ub[cur][:, 2:514], banks[bU], [ps_key(bU)], [("ub", cur)])
                ts("pool", cbuf[cur], ub[cur][:, 2:514], cwT[:, j, 2:3], cbT[:, j:j + 1], ALU.mult, ALU.add,
                   [("ub", cur)] + cwk, [("cbuf", cur)])
                stt(cbuf[cur], ub[cur][:, 1:513], cwT[:, j, 1:2], cbuf[cur], ALU.mult, ALU.add,
                    [("ub", cur), ("ubh", cur), ("cbuf", cur)] + cwk, [("cbuf", cur)])
                stt(cbuf[cur], ub[cur][:, 0:512], cwT[:, j, 0:1], cbuf[cur], ALU.mult, ALU.add,
                    [("ub", cur), ("ubh", cur), ("cbuf", cur)] + cwk, [("cbuf", cur)])
                if pendB[0] is not None:
                    pendB[0]()

                def mkB(cur=cur, bG=bG, jj=jj, cs=cs, blk=blk):
                    def f():
                        act(sbf[cur], cbuf[cur], AF.Silu, [("cbuf", cur)], [("sbf", cur)])
                        tt("dve", actT[:, jj, cs], sbf[cur], banks[bG], ALU.mult, [("sbf", cur), ps_key(bG)],
                           [("actT", jj, blk)])
                    return f
                pendB[0] = mkB()
                it += 1
        if pendB[0] is not None:
            pendB[0]()
            pendB[0] = None
        for tile in range(NT):
            tsl = slice(tile * 128, (tile + 1) * 128)
            for half in range(2):
                hs = slice(half * 512, (half + 1) * 512)
                b = S.bank()
                mmg(banks[b], [(actT[:, jj, tsl], wfd[ws][:, jj, hs]) for jj in range(nj)],
                    [("actT", jj, tile // 4) for jj in range(nj)] + [("w", "wfd%d" % ws)], [ps_key(b)])
                tt("dve", h[:, tile, hs], banks[b], h[:, tile, hs], ALU.add, [ps_key(b), ("h", tile, half)],
                   [("h", tile, half)])
            if gi == len(groups) - 1 and debug != "hD":
                final_norm_tile(tile)

    if debug == "hD":
        dump("h", h, h_keys, [128, NT * D], F32)
        return finish()

    return finish()


_CACHE = {}


def kernel(**inputs):
    if "nc" not in _CACHE:
        _CACHE["nc"] = build()
    nc, _stack = _CACHE["nc"]
    cst = _consts()
    x = np.asarray(inputs["x"], dtype=np.float32)
    mem = np.asarray(inputs["mem"], dtype=np.float32)
    shared = {k: np.ascontiguousarray(np.asarray(v, dtype=np.float32)) for k, v in inputs.items()
              if k not in ("x", "mem")}
    in_maps = []
    for b in range(8):
        m = dict(shared)
        m["x"] = np.ascontiguousarray(x[b])
        m["mem"] = np.ascontiguousarray(mem[b])
        m["cst"] = cst
        in_maps.append(m)
    res = run_bass_kernel_spmd(nc, in_maps, core_ids=list(range(8)))
    return np.stack([np.asarray(r["out"], dtype=np.float32) for r in res.results], axis=0)
```

```python
import contextlib
import numpy as np
import ml_dtypes
import concourse.bass as bass
import concourse.mybir as mybir
from concourse.bass_utils import run_bass_kernel_spmd

F32 = mybir.dt.float32
BF16 = mybir.dt.bfloat16
AF = mybir.ActivationFunctionType
ALU = mybir.AluOpType

T = 2048
D = 1024
NT = 16
NB = 4
EPS = 1e-6
DFF = 2816
NJ = 22
ENG = ("pe", "act", "dve", "pool", "sp")


class Op:
    __slots__ = ("eng", "fn", "deps", "inc", "idx", "dma", "slot", "dval")

    def __init__(self, eng, fn):
        self.eng = eng
        self.fn = fn
        self.deps = {}
        self.inc = False
        self.idx = 0
        self.dma = False
        self.slot = None
        self.dval = 0


class Sched:
    def __init__(self):
        self.ops = {e: [] for e in ENG}
        self.reg = {}
        self.slots = {}
        self.bar = {e: [] for e in ENG}
        self.dma_since = []
        self.nbank = 0

    def bank(self):
        b = self.nbank % 8
        self.nbank += 1
        return b

    def _dep(self, op, p, raw):
        if p is None or p is op:
            return
        cur = op.deps.get(id(p))
        if cur is None or (raw and not cur[1]):
            op.deps[id(p)] = (p, raw)

    def add(self, eng, fn, reads=(), writes=(), dma_slot=None):
        op = Op(eng, fn)
        if dma_slot is not None:
            op.dma = True
            op.slot = dma_slot
            self.slots[dma_slot] = self.slots.get(dma_slot, 0) + 16
            op.dval = self.slots[dma_slot]
            self.dma_since.append(op)
        for p in self.bar[eng]:
            self._dep(op, p, True)
        self.bar[eng] = []
        for k in reads:
            st = self.reg.get(k)
            if st is not None:
                self._dep(op, st[0], True)
        for k in writes:
            st = self.reg.get(k)
            if st is not None:
                self._dep(op, st[0], True)
                for r in st[1].values():
                    self._dep(op, r, False)
                for r in st[2]:
                    self._dep(op, r, True)
        for k in reads:
            st = self.reg.setdefault(k, [None, {}, []])
            if op.dma:
                st[2].append(op)
            else:
                st[1][eng] = op
        for k in writes:
            self.reg[k] = [op, {}, []]
        for (p, raw) in op.deps.values():
            if not p.dma:
                p.inc = True
        self.ops[eng].append(op)
        return op

    def barrier(self):
        last = [self.ops[e][-1] for e in ENG if self.ops[e]]
        pend = last + self.dma_since
        self.dma_since = []
        for e in ENG:
            self.bar[e] = list(pend)
        for p in pend:
            if not p.dma:
                p.inc = True

    def emit(self, nc, stack):
        for e in ENG:
            c = 0
            for op in self.ops[e]:
                if op.inc and not op.dma:
                    c += 1
                    op.idx = c
        esem = {e: stack.enter_context(nc.semaphore("sem_" + e)) for e in ENG}
        dsem = {s: stack.enter_context(nc.semaphore("dsem_" + s)) for s in self.slots}
        ops = self.ops
        slots = self.slots

        def run(engname, eng):
            waited = {}
            for op in ops[engname]:
                for (p, raw) in op.deps.values():
                    if p.dma:
                        key, sem, val = "d" + p.slot, dsem[p.slot], p.dval
                    else:
                        if p.eng == engname and engname == "pe":
                            continue
                        key, sem, val = p.eng, esem[p.eng], p.idx
                    if waited.get(key, 0) >= val:
                        continue
                    eng.wait_ge(sem, val)
                    waited[key] = val
                ins = op.fn(eng)
                if op.dma:
                    ins.then_inc(dsem[op.slot], 16)
                elif op.inc:
                    ins.then_inc(esem[engname], 1)
            if engname == "sp":
                for s, tot in slots.items():
                    eng.wait_ge(dsem[s], tot)

        with nc.Block() as block:
            @block.tensor
            def _(e):
                run("pe", e)

            @block.scalar
            def _(e):
                run("act", e)

            @block.vector
            def _(e):
                run("dve", e)

            @block.gpsimd
            def _(e):
                run("pool", e)

            @block.sync
            def _(e):
                run("sp", e)


def _consts():
    kk = np.arange(128)[:, None]
    cc = np.arange(128)[None, :]
    bias = np.zeros((2, 2, 128, 4, 128), np.float32)
    for g in range(2):
        for r in range(4):
            h = 4 * g + r
            slope = 2.0 ** (-(h + 1))
            dist = 128 + cc - kk
            b = -8.0 * slope * dist
            invalid = (cc >= 64) & (kk < 64)
            bias[g, 0, :, r, :] = np.where(invalid, -30000.0, b)
            dist = np.abs(cc - kk)
            b = -8.0 * slope * dist
            invalid = (cc < 64) & (kk >= 64)
            bias[g, 1, :, r, :] = np.where(invalid, -30000.0, b)
    bias = bias.reshape(4, 128, 512).transpose(1, 0, 2).reshape(128, 2048)
    maskT = ((kk // 64 == cc // 64) & (cc >= kk)).astype(np.float32)
    ident = np.eye(128, dtype=np.float32)
    cst = np.concatenate([bias, maskT, ident], axis=1).astype(ml_dtypes.bfloat16)
    return np.ascontiguousarray(cst)


class Mem:
    def __init__(self, arena, segs):
        self.arena = arena
        self.segs = [list(s) for s in segs]

    def alloc(self, shape, dtype, parts=128):
        n = 1
        for s in shape:
            n *= s
        nbytes = n * (2 if dtype == BF16 else 4)
        nbytes = (nbytes + 31) // 32 * 32
        for s in self.segs:
            if s[1] - s[0] >= nbytes:
                off = s[0]
                s[0] += nbytes
                break
        else:
            raise RuntimeError("SBUF stage alloc overflow: need %d, segs %s" % (nbytes, self.segs))
        v = self.arena[:, off // 4:(off + nbytes) // 4]
        if dtype == BF16:
            v = v.bitcast(BF16)
        v = v[:, 0:n]
        if len(shape) == 2:
            v = v.rearrange("p (a b) -> p a b", a=shape[0])
        elif len(shape) == 3:
            v = v.rearrange("p (a b c) -> p a b c", a=shape[0], b=shape[1])
        if parts != 128:
            v = v[0:parts]
        return v


def build(debug=None):
    nc = bass.Bass("TRN2", target_bir_lowering=False)
    S = Sched()
    stack = contextlib.ExitStack()

    def din(name, shape, dt=F32):
        return nc.dram_tensor(name, list(shape), dt, kind="ExternalInput").ap()

    x_d = din("x", [T, D])
    mem_d = din("mem", [256, D])
    g_mix_d = din("g_mix", [1, D])
    w_in_d = din("w_in", [1, D, 4864])
    lb_d = din("lower_bounds", [2, 512])
    sinks_d = din("attn_sinks", [1, 8])
    g_on_d = din("g_onorm", [1, 128])
    w_a_d = din("w_branch_a", [1, 512, D])
    w_b_d = din("w_branch_b", [1, 512, D])
    w_mo_d = din("w_mix_out", [1, D, D])
    g_cross_d = din("g_cross", [1, D])
    g_mem_d = din("g_mem", [1, D])
    w_cq_d = din("w_cq", [1, D, D])
    w_ckv_d = din("w_ckv", [1, D, 2 * D])
    w_co_d = din("w_co", [1, D, D])
    g_ffn_d = din("g_ffn", [1, D])
    w_fi_d = din("w_ffn_in", [1, D, 2 * DFF])
    cw_d = din("conv_w", [1, 3, DFF])
    cb_d = din("conv_b", [1, DFF])
    w_fd_d = din("w_ffn_down", [1, DFF, D])
    g_fin_d = din("g_final", [D])
    cst_d = din("cst", [128, 2304], BF16)
    out_d = nc.dram_tensor("out", [T, D], F32, kind="ExternalOutput").ap()
    dbg = {}

    NA = 204800 // 4
    arena = stack.enter_context(nc.sbuf_tensor("arena", [128, NA], F32))[:, :]
    banks = [stack.enter_context(nc.psum_tensor("ps%d" % i, [128, 512], F32))[:, :] for i in range(8)]

    P_END = 16384
    H0, H1 = P_END, P_END + 65536
    N0, N1 = H1, H1 + 32768
    F0, F1 = N1, 204800
    pm = Mem(arena, [(0, P_END)])
    cst = pm.alloc([2304], BF16)
    ident = cst[:, 2176:2304]
    maskT = cst[:, 2048:2176]
    ones = pm.alloc([128], BF16)
    cvec = pm.alloc([16], F32)
    stat = pm.alloc([3, 16], F32)
    small = pm.alloc([64], F32)
    gb = [pm.alloc([D], F32), pm.alloc([D], F32)]
    cwT = pm.alloc([NJ, 3], F32)
    cbT = pm.alloc([NJ], F32)
    hm = Mem(arena, [(H0, H1)])
    h = hm.alloc([NT, D], F32)
    nm_ = Mem(arena, [(N0, N1)])
    nT = nm_.alloc([8, T], BF16)

    eps_ap = cvec[:, 0:1]
    one_ap = cvec[:, 1:2]
    ss, ms, rstd = stat[:, 0, :], stat[:, 1, :], stat[:, 2, :]
    l0T, l1T = small[:, 0:4], small[:, 4:8]
    omlb, nomlb = small[:, 8:12], small[:, 12:16]
    gon = small[:, 16:17]
    sinkexp = small[0:64, 24:32]

    misc_n = [0]

    def dma(q, slot, out, in_, reads, writes, **kw):
        if slot == "misc":
            slot = "m%d" % misc_n[0]
            misc_n[0] += 1
        S.add(q, lambda e: e.dma_start(out=out, in_=in_, **kw), reads, writes, dma_slot=slot)

    def act(out, in_, func, reads, writes, bias=None, scale=None, accum_out=None):
        kw = {}
        if bias is not None:
            kw["bias"] = bias
        if scale is not None:
            kw["scale"] = scale
        if accum_out is not None:
            kw["accum_out"] = accum_out
        S.add("act", lambda e: e.activation(out=out, in_=in_, func=func, **kw), reads, writes)

    def tt(eng, out, in0, in1, op, reads, writes):
        S.add(eng, lambda e: e.tensor_tensor(out=out, in0=in0, in1=in1, op=op), reads, writes)

    def ts(eng, out, in0, s1, s2, op0, op1, reads, writes):
        S.add(eng, lambda e: e.tensor_scalar(out=out, in0=in0, scalar1=s1, scalar2=s2, op0=op0, op1=op1),
              reads, writes)

    def stt(out, in0, scalar, in1, op0, op1, reads, writes):
        S.add("dve", lambda e: e.scalar_tensor_tensor(out=out, in0=in0, scalar=scalar, in1=in1, op0=op0, op1=op1),
              reads, writes)

    def cp(eng, out, in_, reads, writes):
        if eng == "act":
            S.add("act", lambda e: e.activation(out=out, in_=in_, func=AF.Copy), reads, writes)
        else:
            S.add(eng, lambda e: e.tensor_copy(out=out, in_=in_), reads, writes)

    def mmg(out, pairs, reads, writes, first=True, last=True, skip=False):
        def fn(e):
            ins = None
            n = len(pairs)
            for i, (l, r) in enumerate(pairs):
                kw = {}
                if skip:
                    kw["skip_group_check"] = True
                ins = e.matmul(out, l, r, start=(first and i == 0), stop=(last and i == n - 1), **kw)
            return ins
        S.add("pe", fn, reads, writes)

    def ps_key(b):
        return ("ps", b)

    def nT_keys(blk):
        return [("nT", 4 * blk + i) for i in range(4)]

    def wload(dst, src, key, after=()):
        dma("pool", "w_" + key, dst, src, list(after), [("w", key)])

    dma("sp", "cst", cst, cst_d, [], [("cst",)])
    S.add("pool", lambda e: e.memset(ones, 1.0), [], [("ones",)])
    S.add("pool", lambda e: e.memset(cvec[:, 0:1], EPS), [], [("cvec",)])
    S.add("pool", lambda e: e.memset(cvec[:, 1:2], 1.0), [("cvec",)], [("cvec",)])

    def load_gb(slot, src2d):
        dma("sp", "gb%d" % slot, gb[slot], src2d.partition_broadcast(128), [], [("gb", slot)])

    def norm_stage(*a, **kw):
        for _ in norm_gen(*a, **kw):
            pass

    def norm_stage(*a, **kw):
        for _ in norm_gen(*a, **kw):
            pass

    def advance(gen, n):
        for _ in range(n):
            try:
                next(gen)
            except StopIteration:
                break

    def norm_gen(n_tiles, gslot, dstT, dst_key, src_dram=None, xt=None, xn=None, junk=None, xtag="xt",
                 xnk="xn", jk="junk"):
        G = 4 if n_tiles >= 4 else n_tiles
        ngrp = n_tiles // G

        def squares(g):
            for i in range(g * G, (g + 1) * G):
                if src_dram is not None:
                    src = xt[i % len(xt)]
                    sk = [(xtag, i % len(xt))]
                    dma("sp", "%s%d" % (xtag, i % len(xt)), src, src_dram[i * 128:(i + 1) * 128, :], [], sk)
                else:
                    src = h[:, i, :]
                    sk = [("h", i, 0), ("h", i, 1)]
                act(junk, src, AF.Square, sk, [(jk,), ("ss", i)], accum_out=ss[:, i:i + 1])
            gs = slice(g * G, (g + 1) * G)
            gk = list(range(g * G, (g + 1) * G))
            act(ms[:, gs], ss[:, gs], AF.Ln, [("ss", i) for i in gk] + [("cvec",)], [("ms", i) for i in gk],
                scale=1.0 / D, bias=eps_ap)
            act(rstd[:, gs], ms[:, gs], AF.Exp, [("ms", i) for i in gk], [("rstd", i) for i in gk], scale=-0.5)

        if src_dram is None:
            squares(0)
        for g in range(ngrp):
            if src_dram is not None:
                squares(g)
            pend = None
            for i in range(g * G, (g + 1) * G):
                if src_dram is not None:
                    src = xt[i % len(xt)]
                    sk = [(xtag, i % len(xt))]
                else:
                    src = h[:, i, :]
                    sk = [("h", i, 0), ("h", i, 1)]
                xnb = xn[i % 2]
                stt(xnb, src, rstd[:, i:i + 1], gb[gslot], ALU.mult, ALU.mult,
                    sk + [("rstd", i), ("gb", gslot)], [(xnk, i % 2)])
                b = S.bank()
                pb = banks[b].bitcast(BF16)

                def tr(e, xnb=xnb, pb=pb):
                    ins = None
                    for c in range(8):
                        ins = e.transpose(pb[:, c * 128:(c + 1) * 128], xnb[:, c * 128:(c + 1) * 128], ident)
                    return ins
                S.add("pe", tr, [(xnk, i % 2), ("cst",)], [ps_key(b)])
                if i == g * G and src_dram is None and g + 1 < ngrp:
                    squares(g + 1)
                if pend is not None:
                    pend()

                def mk(b=b, pb=pb, i=i):
                    def f():
                        cp("act", dstT[:, :, i * 128:(i + 1) * 128], pb.rearrange("p (c n) -> p c n", c=8),
                           [ps_key(b)], [(dst_key, i)])
                    return f
                pend = mk()
            pend()
            yield

    fm = Mem(arena, [(F0, F1), (H0, H1)])
    yT = fm.alloc([8, T], BF16)
    F_REST = fm.segs[0][0]
    xt = [fm.alloc([D], F32) for _ in range(4)]
    whA_pre = arena[:, F_REST // 4:(F_REST + 16384) // 4].bitcast(BF16).rearrange("p (c n) -> p c n", c=8)
    wqkv = fm.alloc([8, 768], BF16)
    wa = fm.alloc([8, D], BF16, parts=64)
    wga = fm.alloc([8, D], BF16)
    w_in = w_in_d[0].rearrange("(c p) n -> p c n", p=128)
    wload(wqkv, w_in[:, :, 0:768], "qkv")
    xn = [fm.alloc([D], BF16), fm.alloc([D], BF16)]
    junk = fm.alloc([D], BF16)
    load_gb(0, g_mix_d)
    norm_stage(NT, 0, nT, "nT", src_dram=x_d, xt=xt, xn=xn, junk=junk)
    wload(wa, w_a_d[0].rearrange("(h d) n -> d h n", d=64), "wa", after=[("nT", 7)])
    wload(wga, w_in[:, :, 2816:3840], "wga", after=[("nT", 7)])

    def dump(name, ap, keys, shape, dt):
        d = nc.dram_tensor("dbg_" + name, list(shape), dt, kind="ExternalOutput").ap()
        if len(ap.shape) == 3:
            dv = d.rearrange("p (c n) -> p c n", c=ap.shape[1])
        else:
            dv = d
        dma("sp", "dbg", dv, ap, keys, [])

    def finish():
        S.emit(nc, stack)
        return nc, stack

    if debug == "nT":
        dump("nT", nT, [("nT", i) for i in range(NT)], [128, 8 * T], BF16)
        return finish()

    qT = [fm.alloc([8, 512], BF16), fm.alloc([8, 512], BF16)]
    kT = fm.alloc([2, T], BF16)
    vv = fm.alloc([NT, 128], BF16)
    oT = fm.alloc([8, 512], BF16, parts=64)
    pT = [fm.alloc([2, 512], BF16), fm.alloc([2, 512], BF16)]
    den = [fm.alloc([512], F32, parts=64), fm.alloc([512], F32, parts=64)]
    sg = [fm.alloc([512], F32), fm.alloc([512], F32)]
    sink_b = fm.alloc([8, 128], F32, parts=64)

    dma("sp", "misc", sinkexp, sinks_d.partition_broadcast(64), [], [("sinkexp",)])
    act(sinkexp, sinkexp, AF.Exp, [("sinkexp",)], [("sinkexp",)])
    cp("dve", sink_b, sinkexp.unsqueeze(2).to_broadcast([64, 8, 128]), [("sinkexp",)], [("sink_b",)])
    for i in range(2):
        S.add("pool", lambda e, i=i: e.memset(qT[i][64:128], 0.0), [], [("qTz", i)])
    S.add("pool", lambda e: e.memset(kT[64:128], 0.0), [], [("kTz",)])

    def biasT(g, kind):
        o = (g * 2 + kind) * 512
        return cst[:, o:o + 512]

    def qk_proj(blk_, hh, eng):
        cs_ = slice(blk_ * 512, (blk_ + 1) * 512)
        b = S.bank()
        col = hh * 64 if hh < 8 else 512 + (hh - 8) * 64
        mmg(banks[b][0:64, :], [(wqkv[:, k, col:col + 64], nT[:, k, cs_]) for k in range(8)],
            nT_keys(blk_) + [("w", "qkv")], [ps_key(b)])
        if hh < 8:
            cp(eng, qT[blk_ % 2][0:64, hh, :], banks[b][0:64, :], [ps_key(b)], [("qT", blk_ % 2, hh)])
        else:
            cp(eng, kT[0:64, hh - 8, cs_], banks[b][0:64, :], [ps_key(b)], [("kT", hh - 8, blk_)])

    QK_SPLIT = [[8, 9], [0], [1], [2, 3], [4], [5], [6], [7]]
    for blk in range(NB):
        cs = slice(blk * 512, (blk + 1) * 512)
        qb = qT[blk % 2]
        if blk == 0:
            for hh in range(10):
                qk_proj(0, hh, "act")
        def v_proj(tile):
            b = S.bank()
            mmg(banks[b][:, 0:128], [(nT[:, k, tile * 128:(tile + 1) * 128], wqkv[:, k, 640:768]) for k in range(8)],
                [("nT", tile), ("w", "qkv")], [ps_key(b)])
            cp("dve", vv[:, tile, :], banks[b][:, 0:128], [ps_key(b)], [("v", tile)])
        if blk == 0:
            for tl in range(4):
                v_proj(tl)
        if blk == 1:
            xk = [("xt", i) for i in range(4)]
            dma("pool", "w_whA0", whA_pre[:, :, 0:512], w_in[:, :, 768 + 1024:768 + 1536], [], [("w", "whA_hi")] + xk)
            dma("pool", "w_whA1", whA_pre[:, :, 512:1024], w_in[:, :, 768 + 512:768 + 1024], [("w", "whA_hi")],
                [("w", "whA_hf")])
        items = [(tl, g) for tl in range(4) for g in range(2)]

        def att_S(it_):
            tl, g = items[it_]
            pp = it_ % 2
            j = 4 * blk + tl
            kts = ([(j - 1, 0)] if j >= 1 else []) + [(j, 1)]
            rq = qb[:, 4 * g:4 * g + 4, tl * 128:(tl + 1) * 128]
            qkeys = [("qT", blk % 2, 4 * g + r) for r in range(4)] + [("qTz", blk % 2), ("kTz",)]
            for idx, (kt, kind) in enumerate(kts):
                b = S.bank()

                def fn(e, b=b, kt=kt, kind=kind, rq=rq, g=g):
                    e.matmul(banks[b].rearrange("p (a n) -> p a n", a=4), kT[:, g, kt * 128:(kt + 1) * 128], rq,
                             start=True, stop=False)
                    return e.matmul(banks[b], ident, biasT(g, kind), start=False, stop=True)
                S.add("pe", fn, qkeys + [("kT", g, kt // 4), ("cst",)], [ps_key(b)])
                act(pT[pp][:, idx, :], banks[b], AF.Exp, [ps_key(b)], [("pT", pp, idx)], scale=0.125)

        def att_PV(it_):
            tl, g = items[it_]
            pp = it_ % 2
            j = 4 * blk + tl
            kts = ([(j - 1, 0)] if j >= 1 else []) + [(j, 1)]
            pk = [("pT", pp, idx) for idx in range(len(kts))]
            bO = S.bank()
            mmg(banks[bO][0:64, :], [(vv[:, kt, g * 64:(g + 1) * 64], pT[pp][:, idx, :])
                                     for idx, (kt, kind) in enumerate(kts)],
                pk + [("v", kt) for kt, _ in kts], [ps_key(bO)])
            bD = S.bank()
            mmg(banks[bD][0:64, :], [(ones[:, 0:64], pT[pp][:, idx, :]) for idx in range(len(kts))],
                pk + [("ones",)], [ps_key(bD)])
            def lnf(e, pp=pp, bD=bD, g=g):
                ins = None
                for r in range(4):
                    ins = e.activation(out=den[pp][:, r * 128:(r + 1) * 128], in_=banks[bD][0:64, r * 128:(r + 1) * 128],
                                       func=AF.Ln, bias=sinkexp[:, 4 * g + r:4 * g + r + 1])
                return ins
            S.add("act", lnf, [ps_key(bD), ("sinkexp",)], [("den", pp)])
            act(den[pp], den[pp], AF.Exp, [("den", pp)], [("den", pp)], scale=-1.0)
            tt("dve", oT[:, 4 * g:4 * g + 4, tl * 128:(tl + 1) * 128],
               banks[bO][0:64, :].rearrange("p (a n) -> p a n", a=4),
               den[pp].rearrange("p (a n) -> p a n", a=4), ALU.mult,
               [ps_key(bO), ("den", pp)], [("oT", g, tl)])

        att_S(0)
        for it_ in range(len(items)):
            if it_ + 1 < len(items):
                att_S(it_ + 1)
            att_PV(it_)
            if blk + 1 < NB:
                for hh in QK_SPLIT[it_]:
                    qk_proj(blk + 1, hh, "dve")
                if it_ % 2 == 1:
                    v_proj(4 * (blk + 1) + it_ // 2)
        okeys = [("oT", g, tl) for g in range(2) for tl in range(4)]
        for fc in range(8):
            fs = slice(fc * 128, (fc + 1) * 128)
            bY = S.bank()
            mmg(banks[bY], [(wa[:, hh, fs], oT[:, hh, :]) for hh in range(8)], okeys + [("w", "wa")], [ps_key(bY)])
            bG = S.bank()
            mmg(banks[bG], [(wga[:, k, fs], nT[:, k, cs]) for k in range(8)],
                nT_keys(blk) + [("w", "wga")], [ps_key(bG)])
            act(sg[fc % 2], banks[bG], AF.Sigmoid, [ps_key(bG)], [("sg", fc % 2)])
            tt("dve", yT[:, fc, cs], banks[bY], sg[fc % 2], ALU.mult, [ps_key(bY), ("sg", fc % 2)], [("yT", fc, blk)])

    yT_keys = [("yT", fc, blk) for fc in range(8) for blk in range(NB)]
    if debug == "yA":
        dump("yT", yT, yT_keys, [128, 8 * T], BF16)
        return finish()

    S.barrier()
    fm = Mem(arena, [(F_REST, F1), (H0, H1)])
    whA = fm.alloc([8, 1024], BF16)
    whB = fm.alloc([8, 1024], BF16)
    wb = fm.alloc([4, D], BF16)
    wgb = fm.alloc([8, D], BF16)
    resetm = pm.alloc([512], F32)
    Scar = fm.alloc([4, 128], F32)
    vc = fm.alloc([4, 512], BF16)
    sgm = fm.alloc([4, 512], F32)
    slu = fm.alloc([4, 512], BF16)
    uT = fm.alloc([4, 512], BF16)
    sg = [fm.alloc([512], F32), fm.alloc([512], F32)]
    sgg = sg
    tmpy = [fm.alloc([512], F32), fm.alloc([512], F32)]
    HB = []
    for i in range(2):
        HB.append(dict(
            A=fm.alloc([512], F32), B=fm.alloc([512], F32), C=fm.alloc([512], F32),
            q_dec=fm.alloc([512], BF16), k_inv=fm.alloc([512], BF16), k_endT=fm.alloc([512], BF16),
            k_end=fm.alloc([4, 128], BF16), aT=fm.alloc([4, 128], BF16),
            Sfp=fm.alloc([7, 128], F32), Sb=fm.alloc([8, 128], BF16), decay=fm.alloc([8, 1], F32)))

    dma("pool", "w_whB0", whB[:, :, 0:512], w_in[:, :, 768 + 1536:768 + 2048], [], [("w", "whB_hg")])
    dma("pool", "w_whB1", whB[:, :, 512:1024], w_in[:, :, 768:768 + 512], [("w", "whB_hg")], [("w", "whB_hq")])
    wload(wgb, w_in[:, :, 3840:4864], "wgb", after=[("w", "whB_hq")])
    wload(wb, w_b_d[0].rearrange("(h v) n -> v h n", v=128), "wb", after=[("w", "whB_hq")])
    dma("sp", "misc", l0T, lb_d[0].rearrange("(h p) -> p h", p=128), [], [("l0T",)], allow_slow_non_contiguous=True)
    dma("sp", "misc", l1T, lb_d[1].rearrange("(h p) -> p h", p=128), [], [("l1T",)], allow_slow_non_contiguous=True)
    dma("sp", "misc", gon, g_on_d[0].rearrange("(p o) -> p o", o=1), [], [("gon",)])
    tt("dve", omlb, l1T, l0T, ALU.subtract, [("l0T",), ("l1T",)], [("omlb",)])
    act(omlb, omlb, AF.Sigmoid, [("omlb",)], [("omlb",)])
    ts("dve", nomlb, omlb, -1.0, None, ALU.mult, ALU.bypass, [("omlb",)], [("nomlb",)])
    S.add("pool", lambda e: e.memset(resetm, 1.0), [], [("resetm",)])
    S.add("pool", lambda e: e.memset(resetm.rearrange("p (c s) -> p c s", s=64)[:, :, 0:1], 0.0),
          [("resetm",)], [("resetm",)])
    S.add("pool", lambda e: e.memset(Scar, 0.0), [], [("Scar", hh) for hh in range(4)])

    def hgrn_stages(blk, hh):
        cs = slice(blk * 512, (blk + 1) * 512)
        s = hh % 2
        X = HB[s]
        A, Bf, C = X["A"], X["B"], X["C"]
        q_dec, k_inv, k_endT, k_end, aT = X["q_dec"], X["k_inv"], X["k_endT"], X["k_end"], X["aT"]
        osq = k_endT
        Sfp, Sb, decay = X["Sfp"], X["Sb"], X["decay"]

        def K(n):
            return (n, s)
        act(A, sgm[:, hh, :], AF.Ln, [("sgm", hh), ("nomlb",), ("cvec",)], [K("A")],
            scale=nomlb[:, hh:hh + 1], bias=one_ap)
        bQ = S.bank()
        mmg(banks[bQ], [(whB[:, k, 512 + hh * 128:512 + (hh + 1) * 128], nT[:, k, cs]) for k in range(8)],
            nT_keys(blk) + [("w", "whB_hq")], [ps_key(bQ)])
        yield
        S.add("dve", lambda e: e.tensor_tensor_scan(out=A, data0=resetm, data1=A, initial=0.0,
                                                    op0=ALU.mult, op1=ALU.add),
              [("resetm",), K("A")], [K("A")])
        yield
        act(Bf, A, AF.Exp, [K("A")], [K("B")])
        act(C, A, AF.Exp, [K("A")], [K("C")], scale=-1.0)
        act(decay, A.rearrange("p (c s) -> p c s", s=64)[:, :, 63:64], AF.Exp, [K("A")], [K("decay")])
        yield
        stt(q_dec, banks[bQ], float(128 ** -0.5), Bf, ALU.mult, ALU.mult, [ps_key(bQ), K("B")], [K("q_dec")])
        stt(k_inv, sgm[:, hh, :], omlb[:, hh:hh + 1], C, ALU.mult, ALU.mult,
            [("sgm", hh), ("omlb",), K("C")], [K("k_inv")])
        yield
        tt("dve", k_endT.rearrange("p (c s) -> p c s", s=64), k_inv.rearrange("p (c s) -> p c s", s=64),
           decay.to_broadcast([128, 8, 64]), ALU.mult, [K("k_inv"), K("decay")], [K("k_endT")])
        yield
        bT_ = S.bank()
        pbT = banks[bT_].bitcast(BF16)

        def trk(e):
            ins = None
            for tl in range(4):
                ins = e.transpose(pbT[:, tl * 128:(tl + 1) * 128], k_endT[:, tl * 128:(tl + 1) * 128], ident)
            return ins
        S.add("pe", trk, [K("k_endT"), ("cst",)], [ps_key(bT_)])
        bA = S.bank()

        def af(e):
            ins = None
            for tl in range(4):
                sl = slice(tl * 128, (tl + 1) * 128)
                ins = e.matmul(banks[bA][:, sl], k_inv[:, sl], q_dec[:, sl], start=True, stop=True)
            return ins
        S.add("pe", af, [K("k_inv"), K("q_dec")], [ps_key(bA)])
        yield
        cp("act", k_end.rearrange("p a n -> p (a n)"), pbT[:, 0:512], [ps_key(bT_)], [K("k_end")])
        tt("dve", aT, banks[bA].rearrange("p (a n) -> p a n", a=4), maskT.unsqueeze(1).to_broadcast([128, 4, 128]),
           ALU.mult, [ps_key(bA), ("cst",)], [K("aT")])
        cp("pool", Sb[:, 0, :], Scar[:, hh, :], [("Scar", hh)], [K("Sb0")])
        yield
        bD = [S.bank(), S.bank()]

        def dsf(e):
            ins = None
            for n in range(8):
                tl, half = n // 2, n % 2
                rows = slice(half * 64, half * 64 + 64)
                ins = e.matmul(banks[bD[half]][:, tl * 128:(tl + 1) * 128], k_end[rows, tl, :],
                               vc[rows, tl, hh * 128:(hh + 1) * 128], start=True, stop=True)
            return ins
        S.add("pe", dsf, [K("k_end")] + [("vc", tl) for tl in range(4)], [ps_key(bD[0]), ps_key(bD[1])])
        yield
        for n in range(8):
            src = Scar[:, hh, :] if n == 0 else Sfp[:, n - 1, :]
            dst = Scar[:, hh, :] if n == 7 else Sfp[:, n, :]
            rk = [("Scar", hh)] if n == 0 else [K("Sfp%d" % (n - 1))]
            wk = [("Scar", hh)] if n == 7 else [K("Sfp%d" % n)]
            stt(dst, src, decay[:, n, :], banks[bD[n % 2]][:, (n // 2) * 128:(n // 2 + 1) * 128],
                ALU.mult, ALU.add, rk + [K("decay"), ps_key(bD[n % 2])], wk)
            if n == 3:
                yield
        cp("dve", Sb[:, 1:8, :], Sfp[:, 0:7, :], [K("Sfp%d" % n) for n in range(7)], [K("Sb1")])
        yield
        bO = S.bank()

        def of(e):
            ins = None
            for tl in range(4):
                sl = slice(tl * 128, (tl + 1) * 128)
                e.matmul(banks[bO][:, sl], vc[:, tl, hh * 128:(hh + 1) * 128], aT[:, tl, :], start=True, stop=False,
                         skip_group_check=True)
                for cc in range(2):
                    n = 2 * tl + cc
                    s2 = slice(tl * 128 + cc * 64, tl * 128 + cc * 64 + 64)
                    ins = e.matmul(banks[bO][:, s2], Sb[:, n, :], q_dec[:, s2], start=False, stop=(cc == 1),
                                   skip_group_check=True)
            return ins
        S.add("pe", of, [K("aT"), K("q_dec"), K("Sb0"), K("Sb1")] + [("vc", tl) for tl in range(4)], [ps_key(bO)])
        yield
        act(osq, banks[bO], AF.Square, [ps_key(bO)], [K("k_endT")])
        yield
        bN = S.bank()
        mmg(banks[bN], [(ones, osq)], [K("k_endT"), ("ones",)], [ps_key(bN)])
        yield
        act(Bf, banks[bN], AF.Ln, [ps_key(bN), ("cvec",)], [K("B")], scale=1.0 / 128, bias=eps_ap)
        act(Bf, Bf, AF.Exp, [K("B")], [K("B")], scale=-0.5)
        yield
        tt("dve", C, banks[bO], Bf, ALU.mult, [ps_key(bO), K("B")], [K("C")])
        tt("dve", uT[:, hh, :], C, slu[:, hh, :], ALU.mult, [K("C"), ("slu", hh)], [("uT", hh)])
        yield

    sgB = gb[0][:, :].bitcast(BF16)
    sgB2 = gb[1][:, :].bitcast(BF16)

    def sgB_ap(fc):
        src = sgB if fc < 4 else sgB2
        return src[:, (fc % 4) * 512:(fc % 4 + 1) * 512]

    def sig_from_psum(out, bank, neg, tmp, tkey, rkeys, wkeys):
        act(tmp, banks[bank], AF.Exp, [ps_key(bank)] + rkeys, [tkey], scale=(1.0 if neg else -1.0))
        act(tmp, tmp, AF.Ln, [tkey, ("cvec",)], [tkey], bias=one_ap)
        act(out, tmp, AF.Exp, [tkey], wkeys, scale=-1.0)

    act_next = []

    def p1_chunks(blk, hh):
        cs = slice(blk * 512, (blk + 1) * 512)

        def c_f():
            bF = S.bank()
            mmg(banks[bF], [(whA[:, k, 512 + hh * 128:512 + (hh + 1) * 128], nT[:, k, cs]) for k in range(8)],
                nT_keys(blk) + [("w", "whA_hf")], [ps_key(bF)])
            act_next.append(lambda: sig_from_psum(sgm[:, hh, :], bF, True, sg[0], ("sg", 0), [], [("sgm", hh)]))

        def c_g():
            bG = S.bank()
            mmg(banks[bG], [(whB[:, k, hh * 128:(hh + 1) * 128], nT[:, k, cs]) for k in range(8)],
                nT_keys(blk) + [("w", "whB_hg")], [ps_key(bG)])

            def a():
                sig_from_psum(sg[1], bG, False, sg[1], ("sg", 1), [], [("sg", 1)])
                deferred.append(lambda: stt(slu[:, hh, :], banks[bG], gon, sg[1], ALU.mult, ALU.mult,
                                            [ps_key(bG), ("sg", 1), ("gon",)], [("slu", hh)]))
            act_next.append(a)
        return [c_f, c_g]

    def gate_chunk(blk, fc):
        cs = slice(blk * 512, (blk + 1) * 512)

        def c():
            bG = S.bank()
            mmg(banks[bG], [(wgb[:, k, fc * 128:(fc + 1) * 128], nT[:, k, cs]) for k in range(8)],
                nT_keys(blk) + [("w", "wgb")], [ps_key(bG)])
            act_next.append(lambda: sig_from_psum(sgB_ap(fc), bG, False, sg[fc % 2], ("sg", fc % 2), [],
                                                  [("sgB", fc)]))
        return c

    deferred = []

    def flush_deferred():
        n = len(deferred)
        for _ in range(n):
            deferred.pop(0)()
        n = len(act_next)
        for _ in range(n):
            act_next.pop(0)()

    def run_pair(blk, pair, fillers):
        gens = [hgrn_stages(blk, 2 * pair), hgrn_stages(blk, 2 * pair + 1)]
        alive = True
        rnd = 0
        while alive:
            alive = False
            for g_ in gens:
                try:
                    next(g_)
                    alive = True
                except StopIteration:
                    pass
            rnd += 1
            flush_deferred()
            if fillers and rnd >= 2:
                fillers.pop(0)()
        while fillers:
            flush_deferred()
            fillers.pop(0)()
        flush_deferred()
        flush_deferred()

    nreg = Mem(arena, [(N0, N1)])
    wmo = nreg.alloc([8, D], BF16)
    wckK = nreg.alloc([8, D], BF16)
    w_mo = w_mo_d[0].rearrange("(c p) n -> p c n", p=128)

    def wmo_prefetch():
        dma("pool", "w_wmo", wmo, w_mo, [], [("w", "wmo")] + [("nT", i) for i in range(NT)])

    for blk in range(NB):
        cs = slice(blk * 512, (blk + 1) * 512)
        for tl in range(4):
            tile = 4 * blk + tl
            b = S.bank()
            mmg(banks[b], [(nT[:, k, tile * 128:(tile + 1) * 128], whA[:, k, 0:512]) for k in range(8)],
                [("nT", tile), ("w", "whA_hi")], [ps_key(b)])
            cp("dve", vc[:, tl, :], banks[b], [ps_key(b)], [("vc", tl)])
        if blk == 0:
            pc = [p1_chunks(0, hh) for hh in range(2)]
            for c in (pc[0][0], pc[1][0], pc[0][1], pc[1][1]):
                c()
                flush_deferred()
            flush_deferred()
            flush_deferred()
        f0 = [gate_chunk(blk, fc) for fc in range(4)]
        p1 = p1_chunks(blk, 2) + p1_chunks(blk, 3)
        f0 = [f0[0], p1[0], f0[1], p1[1], f0[2], p1[2], f0[3], p1[3]]
        run_pair(blk, 0, f0)
        f1 = [gate_chunk(blk, fc) for fc in range(4, 8)]
        if blk + 1 < NB:
            p1 = p1_chunks(blk + 1, 0) + p1_chunks(blk + 1, 1)
            f1 = [f1[0], p1[0], f1[1], p1[1], f1[2], p1[2], f1[3], p1[3]]
        if blk == NB - 1:
            f1 = f1 + [wmo_prefetch]
        run_pair(blk, 1, f1)
        for fc in range(8):
            fs = slice(fc * 128, (fc + 1) * 128)
            bY = S.bank()
            mmg(banks[bY], [(wb[:, hh, fs], uT[:, hh, :]) for hh in range(4)],
                [("uT", hh) for hh in range(4)] + [("w", "wb")], [ps_key(bY)])
            tt("dve", tmpy[fc % 2], banks[bY], sgB_ap(fc), ALU.mult, [ps_key(bY), ("sgB", fc)],
               [("tmpy", fc % 2)])
            tt("pool", yT[:, fc, cs], yT[:, fc, cs], tmpy[fc % 2], ALU.add, [("yT", fc, blk), ("tmpy", fc % 2)],
               [("yT", fc, blk)])

    if debug == "yB":
        dump("yT", yT, yT_keys, [128, 8 * T], BF16)
        return finish()

    S.barrier()
    fm = Mem(arena, [(F_REST, F1)])
    xt = [fm.alloc([D], F32), fm.alloc([D], F32)]
    wckV = fm.alloc([8, D], BF16)
    KV0 = fm.segs[0][0]
    KT = fm.alloc([8, 256], BF16)
    Vm = fm.alloc([2, D], BF16)
    KV1 = fm.segs[0][0]
    nmT = fm.alloc([8, 256], BF16)
    xtm = [fm.alloc([D], F32), fm.alloc([D], F32)]
    xnm = [fm.alloc([D], BF16), fm.alloc([D], BF16)]
    junkm = fm.alloc([D], BF16)
    load_gb(0, g_mem_d)
    w_ckv = w_ckv_d[0].rearrange("(c p) n -> p c n", p=128)
    wload(wckK, w_ckv[:, :, 0:D], "wckK")
    wload(wckV, w_ckv[:, :, D:2 * D], "wckV", after=[("w", "wckK")])

    def mem_kv():
        norm_stage(2, 0, nmT, "nmT", src_dram=mem_d, xt=xtm, xn=xnm, junk=junkm, xtag="xm", xnk="xnm", jk="junkm")
        nm_keys = [("nmT", 0), ("nmT", 1)]
        for c8 in range(8):
            b = S.bank()
            mmg(banks[b][:, 0:256], [(wckK[:, k, c8 * 128:(c8 + 1) * 128], nmT[:, k, :]) for k in range(8)],
                nm_keys + [("w", "wckK")], [ps_key(b)])
            cp("act", KT[:, c8, :], banks[b][:, 0:256], [ps_key(b)], [("KT", c8)])
        for mt in range(2):
            for half in range(2):
                b = S.bank()
                mmg(banks[b], [(nmT[:, k, mt * 128:(mt + 1) * 128], wckV[:, k, half * 512:(half + 1) * 512])
                               for k in range(8)], nm_keys + [("w", "wckV")], [ps_key(b)])
                cp("act", Vm[:, mt, half * 512:(half + 1) * 512], banks[b], [ps_key(b)], [("Vm", mt, half)])

    for tile in range(NT):
        tsl = slice(tile * 128, (tile + 1) * 128)
        if tile == 10 and debug != "hB":
            mem_kv()
        dma("sp", "xt%d" % (tile % 2), xt[tile % 2], x_d[tsl, :], [], [("xt", tile % 2)])
        for half in range(2):
            hs = slice(half * 512, (half + 1) * 512)
            b = S.bank()
            mmg(banks[b], [(yT[:, k, tsl], wmo[:, k, hs]) for k in range(8)],
                [("yT", k, tile // 4) for k in range(8)] + [("w", "wmo")], [ps_key(b)])
            tt("dve", h[:, tile, hs], banks[b], xt[tile % 2][:, hs], ALU.add, [ps_key(b), ("xt", tile % 2)],
               [("h", tile, half)])

    h_keys = [("h", i, hf) for i in range(NT) for hf in range(2)]
    if debug == "hB":
        dump("h", h, h_keys, [128, NT * D], F32)
        return finish()


    S.barrier()
    fm = Mem(arena, [(F0, KV0), (KV1, F1)])
    xn = [fm.alloc([D], BF16), fm.alloc([D], BF16)]
    junk = fm.alloc([D], BF16)
    load_gb(1, g_cross_d)
    wcq = fm.alloc([8, D], BF16)
    wco = fm.alloc([8, D], BF16)
    qTc2 = [fm.alloc([8, 512], BF16), fm.alloc([8, 512], BF16)]
    oTc = fm.alloc([8, 512], BF16)
    pTc = [fm.alloc([2, 512], BF16), fm.alloc([2, 512], BF16)]
    rec = [fm.alloc([512], F32), fm.alloc([512], F32)]
    wload(wcq, w_cq_d[0].rearrange("(c p) n -> p c n", p=128), "wcq")
    wload(wco, w_co_d[0].rearrange("(c p) n -> p c n", p=128), "wco", after=[("w", "wcq")])
    ngen = norm_gen(NT, 1, nT, "nT", xn=xn, junk=junk)
    def qproj(blk_, c8, eng):
        cs_ = slice(blk_ * 512, (blk_ + 1) * 512)
        b = S.bank()
        mmg(banks[b], [(wcq[:, k, c8 * 128:(c8 + 1) * 128], nT[:, k, cs_]) for k in range(8)],
            nT_keys(blk_) + [("w", "wcq")], [ps_key(b)])
        cp(eng, qTc2[blk_ % 2][:, c8, :], banks[b], [ps_key(b)], [("qTc", blk_ % 2, c8)])

    advance(ngen, 2)
    for c8 in range(8):
        qproj(0, c8, "act" if c8 % 2 else "dve")
    for blk in range(NB):
        cs = slice(blk * 512, (blk + 1) * 512)
        qTc = qTc2[blk % 2]
        if blk + 1 < NB:
            advance(ngen, 1)
        def x_S(head):
            pp = head % 2
            for mt in range(2):
                b = S.bank()
                mmg(banks[b], [(KT[:, head * 2 + dc, mt * 128:(mt + 1) * 128], qTc[:, head * 2 + dc, :]) for dc in range(2)],
                    [("KT", head * 2), ("KT", head * 2 + 1), ("qTc", blk % 2, head * 2), ("qTc", blk % 2, head * 2 + 1)],
                    [ps_key(b)])
                act(pTc[pp][:, mt, :], banks[b], AF.Exp, [ps_key(b)], [("pTc", pp, mt)], scale=1.0 / 16)

        def x_PV(head):
            pp = head % 2
            pk = [("pTc", pp, 0), ("pTc", pp, 1)]
            bD = S.bank()
            mmg(banks[bD], [(ones, pTc[pp][:, mt, :]) for mt in range(2)], pk + [("ones",)], [ps_key(bD)])
            act(rec[pp], banks[bD], AF.Ln, [ps_key(bD)], [("rec", pp)])
            act(rec[pp], rec[pp], AF.Exp, [("rec", pp)], [("rec", pp)], scale=-1.0)
            for dc in range(2):
                bO = S.bank()
                c0 = head * 256 + dc * 128
                mmg(banks[bO], [(Vm[:, mt, c0:c0 + 128], pTc[pp][:, mt, :]) for mt in range(2)],
                    pk + [("Vm", mt, c0 // 512) for mt in range(2)], [ps_key(bO)])
                tt("dve", oTc[:, head * 2 + dc, :], banks[bO], rec[pp], ALU.mult, [ps_key(bO), ("rec", pp)],
                   [("oTc", head * 2 + dc)])

        x_S(0)
        for head in range(4):
            if head + 1 < 4:
                x_S(head + 1)
            x_PV(head)
            if blk + 1 < NB:
                qproj(blk + 1, 2 * head, "dve")
                qproj(blk + 1, 2 * head + 1, "dve")
        for tl in range(4):
            tile = 4 * blk + tl
            for half in range(2):
                hs = slice(half * 512, (half + 1) * 512)
                b = S.bank()
                mmg(banks[b], [(oTc[:, k, tl * 128:(tl + 1) * 128], wco[:, k, hs]) for k in range(8)],
                    [("oTc", k) for k in range(8)] + [("w", "wco")], [ps_key(b)])
                tt("dve", h[:, tile, hs], banks[b], h[:, tile, hs], ALU.add, [ps_key(b), ("h", tile, half)],
                   [("h", tile, half)])

    if debug == "hC":
        dump("h", h, h_keys, [128, NT * D], F32)
        return finish()

    S.barrier()
    fm = Mem(arena, [(F0, F1)])
    wfi = [fm.alloc([8, 1024], BF16), fm.alloc([8, 1024], BF16)]
    wfd = [fm.alloc([4, D], BF16), fm.alloc([4, D], BF16)]
    actT = fm.alloc([4, T], BF16)
    ub = [fm.alloc([520], F32), fm.alloc([520], F32)]
    cbuf = [fm.alloc([512], F32), fm.alloc([512], F32)]
    sbf = [fm.alloc([512], F32), fm.alloc([512], F32)]
    xnbuf = fm.alloc([D], F32)
    xn = [xnbuf[:, 0:512].bitcast(BF16), xnbuf[:, 512:1024].bitcast(BF16)]
    junk = fm.alloc([D], BF16)
    load_gb(0, g_ffn_d)
    load_gb(1, g_fin_d.rearrange("(o n) -> o n", o=1))
    ot = [fm.alloc([D], F32), xnbuf]
    for w in range(3):
        dma("sp", "misc", cwT[:, :, w], cw_d[0, w].rearrange("(j p) -> p j", p=128), [], [("cwT", w)],
            allow_slow_non_contiguous=True)
    dma("sp", "misc", cbT, cb_d[0].rearrange("(j p) -> p j", p=128), [], [("cbT",)], allow_slow_non_contiguous=True)
    w_fi = w_fi_d[0].rearrange("(c p) n -> p c n", p=128)
    groups = [(0, 2), (2, 4), (6, 4), (10, 4), (14, 4), (18, 4)]

    def ffn_load(gi):
        j0, nj = groups[gi]
        ws = gi % 2
        wload(wfi[ws][:, :, 0:nj * 128], w_fi[:, :, j0 * 128:(j0 + nj) * 128], "wfiu%d" % ws)
        wload(wfi[ws][:, :, 512:512 + nj * 128], w_fi[:, :, DFF + j0 * 128:DFF + (j0 + nj) * 128], "wfig%d" % ws)
        wload(wfd[ws][:, 0:nj, :], w_fd_d[0][j0 * 128:(j0 + nj) * 128, :].rearrange("(j p) n -> p j n", p=128),
              "wfd%d" % ws, after=[("w", "wfiu%d" % ws), ("w", "wfig%d" % ws)])

    fin_pend = [None]

    def final_norm_tile(i):
        src = h[:, i, :]
        skeys = [("h", i, 0), ("h", i, 1)]
        act(junk, src, AF.Square, skeys, [("junk",), ("ss", i)], accum_out=ss[:, i:i + 1])
        act(ms[:, i:i + 1], ss[:, i:i + 1], AF.Ln, [("ss", i), ("cvec",)], [("ms", i)], scale=1.0 / D, bias=eps_ap)
        act(rstd[:, i:i + 1], ms[:, i:i + 1], AF.Exp, [("ms", i)], [("rstd", i)], scale=-0.5)
        if fin_pend[0] is not None:
            fin_pend[0]()

        def second():
            stt(ot[i % 2], src, rstd[:, i:i + 1], gb[1], ALU.mult, ALU.mult,
                skeys + [("rstd", i), ("gb", 1)], [("ot", i % 2)])
            dma("sp", "out%d" % (i % 2), out_d[i * 128:(i + 1) * 128, :], ot[i % 2], [("ot", i % 2)], [])
        fin_pend[0] = second
        if i == NT - 1:
            second()
            fin_pend[0] = None

    ffn_load(0)
    ngen = norm_gen(NT, 0, nT, "nT", xn=xn, junk=junk)
    cwk = [("cwT", w) for w in range(3)] + [("cbT",)]
    it = 0
    pendB = [None]
    for gi, (j0, nj) in enumerate(groups):
        ws = gi % 2
        if gi + 1 < len(groups):
            ffn_load(gi + 1)
        for jj in range(nj):
            j = j0 + jj
            for blk in range(NB):
                cs = slice(blk * 512, (blk + 1) * 512)
                cur, prv = it % 2, (it + 1) % 2
                if gi == 0 and jj == 0:
                    advance(ngen, 2 if blk == 0 else 1)
                bU = S.bank()
                mmg(banks[bU], [(wfi[ws][:, k, jj * 128:(jj + 1) * 128], nT[:, k, cs]) for k in range(8)],
                    nT_keys(blk) + [("w", "wfiu%d" % ws)], [ps_key(bU)])
                bG = S.bank()
                mmg(banks[bG], [(wfi[ws][:, k, 512 + jj * 128:512 + (jj + 1) * 128], nT[:, k, cs]) for k in range(8)],
                    nT_keys(blk) + [("w", "wfig%d" % ws)], [ps_key(bG)])
                if blk == 0:
                    S.add("pool", lambda e, cur=cur: e.memset(ub[cur][:, 0:2], 0.0), [], [("ubh", cur)])
                else:
                    cp("pool", ub[cur][:, 0:2], ub[prv][:, 512:514], [("ub", prv)], [("ubh", cur)])
                cp("act", ub[cur][:, 2:514], banks[bU], [ps_key(bU)], [("ub", cur)])
                ts("pool", cbuf[cur], ub[cur][:, 2:514], cwT[:, j, 2:3], cbT[:, j:j + 1], ALU.mult, ALU.add,
                   [("ub", cur)] + cwk, [("cbuf", cur)])
                stt(cbuf[cur], ub[cur][:, 1:513], cwT[:, j, 1:2], cbuf[cur], ALU.mult, ALU.add,
                    [("ub", cur), ("ubh", cur), ("cbuf", cur)] + cwk, [("cbuf", cur)])
                stt(cbuf[cur], ub[cur][:, 0:512], cwT[:, j, 0:1], cbuf[cur], ALU.mult, ALU.add,
                    [("ub", cur), ("ubh", cur), ("cbuf", cur)] + cwk, [("cbuf", cur)])
                if pendB[0] is not None:
                    pendB[0]()

                def mkB(cur=cur, bG=bG, jj=jj, cs=cs, blk=blk):
                    def f():
                        act(sbf[cur], cbuf[cur], AF.Silu, [("cbuf", cur)], [("sbf", cur)])
                        tt("dve", actT[:, jj, cs], sbf[cur], banks[bG], ALU.mult, [("sbf", cur), ps_key(bG)],
                           [("actT", jj, blk)])
                    return f
                pendB[0] = mkB()
                it += 1
        if pendB[0] is not None:
            pendB[0]()
            pendB[0] = None
        for tile in range(NT):
            tsl = slice(tile * 128, (tile + 1) * 128)
            for half in range(2):
                hs = slice(half * 512, (half + 1) * 512)
                b = S.bank()
                mmg(banks[b], [(actT[:, jj, tsl], wfd[ws][:, jj, hs]) for jj in range(nj)],
                    [("actT", jj, tile // 4) for jj in range(nj)] + [("w", "wfd%d" % ws)], [ps_key(b)])
                tt("dve", h[:, tile, hs], banks[b], h[:, tile, hs], ALU.add, [ps_key(b), ("h", tile, half)],
                   [("h", tile, half)])
            if gi == len(groups) - 1 and debug != "hD":
                final_norm_tile(tile)

    if debug == "hD":
        dump("h", h, h_keys, [128, NT * D], F32)
        return finish()

    return finish()


_CACHE = {}


def kernel(**inputs):
    if "nc" not in _CACHE:
        _CACHE["nc"] = build()
    nc, _stack = _CACHE["nc"]
    cst = _consts()
    x = np.asarray(inputs["x"], dtype=np.float32)
    mem = np.asarray(inputs["mem"], dtype=np.float32)
    shared = {k: np.ascontiguousarray(np.asarray(v, dtype=np.float32)) for k, v in inputs.items()
              if k not in ("x", "mem")}
    in_maps = []
    for b in range(8):
        m = dict(shared)
        m["x"] = np.ascontiguousarray(x[b])
        m["mem"] = np.ascontiguousarray(mem[b])
        m["cst"] = cst
        in_maps.append(m)
    res = run_bass_kernel_spmd(nc, in_maps, core_ids=list(range(8)))
    return np.stack([np.asarray(r["out"], dtype=np.float32) for r in res.results], axis=0)
```

```python
import contextlib
import numpy as np
import ml_dtypes
import concourse.bass as bass
import concourse.mybir as mybir
from concourse.bass_utils import run_bass_kernel_spmd

F32 = mybir.dt.float32
BF16 = mybir.dt.bfloat16
AF = mybir.ActivationFunctionType
ALU = mybir.AluOpType

T = 2048
D = 1024
NT = 16
NB = 4
EPS = 1e-6
DFF = 2816
NJ = 22
ENG = ("pe", "act", "dve", "pool", "sp")


class Op:
    __slots__ = ("eng", "fn", "deps", "inc", "idx", "dma", "slot", "dval")

    def __init__(self, eng, fn):
        self.eng = eng
        self.fn = fn
        self.deps = {}
        self.inc = False
        self.idx = 0
        self.dma = False
        self.slot = None
        self.dval = 0


class Sched:
    def __init__(self):
        self.ops = {e: [] for e in ENG}
        self.reg = {}
        self.slots = {}
        self.bar = {e: [] for e in ENG}
        self.dma_since = []
        self.nbank = 0

    def bank(self):
        b = self.nbank % 8
        self.nbank += 1
        return b

    def _dep(self, op, p, raw):
        if p is None or p is op:
            return
        cur = op.deps.get(id(p))
        if cur is None or (raw and not cur[1]):
            op.deps[id(p)] = (p, raw)

    def add(self, eng, fn, reads=(), writes=(), dma_slot=None):
        op = Op(eng, fn)
        if dma_slot is not None:
            op.dma = True
            op.slot = dma_slot
            self.slots[dma_slot] = self.slots.get(dma_slot, 0) + 16
            op.dval = self.slots[dma_slot]
            self.dma_since.append(op)
        for p in self.bar[eng]:
            self._dep(op, p, True)
        self.bar[eng] = []
        for k in reads:
            st = self.reg.get(k)
            if st is not None:
                self._dep(op, st[0], True)
        for k in writes:
            st = self.reg.get(k)
            if st is not None:
                self._dep(op, st[0], True)
                for r in st[1].values():
                    self._dep(op, r, False)
                for r in st[2]:
                    self._dep(op, r, True)
        for k in reads:
            st = self.reg.setdefault(k, [None, {}, []])
            if op.dma:
                st[2].append(op)
            else:
                st[1][eng] = op
        for k in writes:
            self.reg[k] = [op, {}, []]
        for (p, raw) in op.deps.values():
            if not p.dma:
                p.inc = True
        self.ops[eng].append(op)
        return op

    def barrier(self):
        last = [self.ops[e][-1] for e in ENG if self.ops[e]]
        pend = last + self.dma_since
        self.dma_since = []
        for e in ENG:
            self.bar[e] = list(pend)
        for p in pend:
            if not p.dma:
                p.inc = True

    def emit(self, nc, stack):
        for e in ENG:
            c = 0
            for op in self.ops[e]:
                if op.inc and not op.dma:
                    c += 1
                    op.idx = c
        esem = {e: stack.enter_context(nc.semaphore("sem_" + e)) for e in ENG}
        dsem = {s: stack.enter_context(nc.semaphore("dsem_" + s)) for s in self.slots}
        ops = self.ops
        slots = self.slots

        def run(engname, eng):
            waited = {}
            for op in ops[engname]:
                for (p, raw) in op.deps.values():
                    if p.dma:
                        key, sem, val = "d" + p.slot, dsem[p.slot], p.dval
                    else:
                        if p.eng == engname and engname == "pe":
                            continue
                        key, sem, val = p.eng, esem[p.eng], p.idx
                    if waited.get(key, 0) >= val:
                        continue
                    eng.wait_ge(sem, val)
                    waited[key] = val
                ins = op.fn(eng)
                if op.dma:
                    ins.then_inc(dsem[op.slot], 16)
                elif op.inc:
                    ins.then_inc(esem[engname], 1)
            if engname == "sp":
                for s, tot in slots.items():
                    eng.wait_ge(dsem[s], tot)

        with nc.Block() as block:
            @block.tensor
            def _(e):
                run("pe", e)

            @block.scalar
            def _(e):
                run("act", e)

            @block.vector
            def _(e):
                run("dve", e)

            @block.gpsimd
            def _(e):
                run("pool", e)

            @block.sync
            def _(e):
                run("sp", e)


def _consts():
    kk = np.arange(128)[:, None]
    cc = np.arange(128)[None, :]
    bias = np.zeros((2, 2, 128, 4, 128), np.float32)
    for g in range(2):
        for r in range(4):
            h = 4 * g + r
            slope = 2.0 ** (-(h + 1))
            dist = 128 + cc - kk
            b = -8.0 * slope * dist
            invalid = (cc >= 64) & (kk < 64)
            bias[g, 0, :, r, :] = np.where(invalid, -30000.0, b)
            dist = np.abs(cc - kk)
            b = -8.0 * slope * dist
            invalid = (cc < 64) & (kk >= 64)
            bias[g, 1, :, r, :] = np.where(invalid, -30000.0, b)
    bias = bias.reshape(4, 128, 512).transpose(1, 0, 2).reshape(128, 2048)
    maskT = ((kk // 64 == cc // 64) & (cc >= kk)).astype(np.float32)
    ident = np.eye(128, dtype=np.float32)
    cst = np.concatenate([bias, maskT, ident], axis=1).astype(ml_dtypes.bfloat16)
    return np.ascontiguousarray(cst)


class Mem:
    def __init__(self, arena, segs):
        self.arena = arena
        self.segs = [list(s) for s in segs]

    def alloc(self, shape, dtype, parts=128):
        n = 1
        for s in shape:
            n *= s
        nbytes = n * (2 if dtype == BF16 else 4)
        nbytes = (nbytes + 31) // 32 * 32
        for s in self.segs:
            if s[1] - s[0] >= nbytes:
                off = s[0]
                s[0] += nbytes
                break
        else:
            raise RuntimeError("SBUF stage alloc overflow: need %d, segs %s" % (nbytes, self.segs))
        v = self.arena[:, off // 4:(off + nbytes) // 4]
        if dtype == BF16:
            v = v.bitcast(BF16)
        v = v[:, 0:n]
        if len(shape) == 2:
            v = v.rearrange("p (a b) -> p a b", a=shape[0])
        elif len(shape) == 3:
            v = v.rearrange("p (a b c) -> p a b c", a=shape[0], b=shape[1])
        if parts != 128:
            v = v[0:parts]
        return v


def build(debug=None):
    nc = bass.Bass("TRN2", target_bir_lowering=False)
    S = Sched()
    stack = contextlib.ExitStack()

    def din(name, shape, dt=F32):
        return nc.dram_tensor(name, list(shape), dt, kind="ExternalInput").ap()

    x_d = din("x", [T, D])
    mem_d = din("mem", [256, D])
    g_mix_d = din("g_mix", [1, D])
    w_in_d = din("w_in", [1, D, 4864])
    lb_d = din("lower_bounds", [2, 512])
    sinks_d = din("attn_sinks", [1, 8])
    g_on_d = din("g_onorm", [1, 128])
    w_a_d = din("w_branch_a", [1, 512, D])
    w_b_d = din("w_branch_b", [1, 512, D])
    w_mo_d = din("w_mix_out", [1, D, D])
    g_cross_d = din("g_cross", [1, D])
    g_mem_d = din("g_mem", [1, D])
    w_cq_d = din("w_cq", [1, D, D])
    w_ckv_d = din("w_ckv", [1, D, 2 * D])
    w_co_d = din("w_co", [1, D, D])
    g_ffn_d = din("g_ffn", [1, D])
    w_fi_d = din("w_ffn_in", [1, D, 2 * DFF])
    cw_d = din("conv_w", [1, 3, DFF])
    cb_d = din("conv_b", [1, DFF])
    w_fd_d = din("w_ffn_down", [1, DFF, D])
    g_fin_d = din("g_final", [D])
    cst_d = din("cst", [128, 2304], BF16)
    out_d = nc.dram_tensor("out", [T, D], F32, kind="ExternalOutput").ap()
    dbg = {}

    NA = 204800 // 4
    arena = stack.enter_context(nc.sbuf_tensor("arena", [128, NA], F32))[:, :]
    banks = [stack.enter_context(nc.psum_tensor("ps%d" % i, [128, 512], F32))[:, :] for i in range(8)]

    P_END = 16384
    H0, H1 = P_END, P_END + 65536
    N0, N1 = H1, H1 + 32768
    F0, F1 = N1, 204800
    pm = Mem(arena, [(0, P_END)])
    cst = pm.alloc([2304], BF16)
    ident = cst[:, 2176:2304]
    maskT = cst[:, 2048:2176]
    ones = pm.alloc([128], BF16)
    cvec = pm.alloc([16], F32)
    stat = pm.alloc([3, 16], F32)
    small = pm.alloc([64], F32)
    gb = [pm.alloc([D], F32), pm.alloc([D], F32)]
    cwT = pm.alloc([NJ, 3], F32)
    cbT = pm.alloc([NJ], F32)
    hm = Mem(arena, [(H0, H1)])
    h = hm.alloc([NT, D], F32)
    nm_ = Mem(arena, [(N0, N1)])
    nT = nm_.alloc([8, T], BF16)

    eps_ap = cvec[:, 0:1]
    one_ap = cvec[:, 1:2]
    ss, ms, rstd = stat[:, 0, :], stat[:, 1, :], stat[:, 2, :]
    l0T, l1T = small[:, 0:4], small[:, 4:8]
    omlb, nomlb = small[:, 8:12], small[:, 12:16]
    gon = small[:, 16:17]
    sinkexp = small[0:64, 24:32]

    misc_n = [0]

    def dma(q, slot, out, in_, reads, writes, **kw):
        if slot == "misc":
            slot = "m%d" % misc_n[0]
            misc_n[0] += 1
        S.add(q, lambda e: e.dma_start(out=out, in_=in_, **kw), reads, writes, dma_slot=slot)

    def act(out, in_, func, reads, writes, bias=None, scale=None, accum_out=None):
        kw = {}
        if bias is not None:
            kw["bias"] = bias
        if scale is not None:
            kw["scale"] = scale
        if accum_out is not None:
            kw["accum_out"] = accum_out
        S.add("act", lambda e: e.activation(out=out, in_=in_, func=func, **kw), reads, writes)

    def tt(eng, out, in0, in1, op, reads, writes):
        S.add(eng, lambda e: e.tensor_tensor(out=out, in0=in0, in1=in1, op=op), reads, writes)

    def ts(eng, out, in0, s1, s2, op0, op1, reads, writes):
        S.add(eng, lambda e: e.tensor_scalar(out=out, in0=in0, scalar1=s1, scalar2=s2, op0=op0, op1=op1),
              reads, writes)

    def stt(out, in0, scalar, in1, op0, op1, reads, writes):
        S.add("dve", lambda e: e.scalar_tensor_tensor(out=out, in0=in0, scalar=scalar, in1=in1, op0=op0, op1=op1),
              reads, writes)

    def cp(eng, out, in_, reads, writes):
        if eng == "act":
            S.add("act", lambda e: e.activation(out=out, in_=in_, func=AF.Copy), reads, writes)
        else:
            S.add(eng, lambda e: e.tensor_copy(out=out, in_=in_), reads, writes)

    def mmg(out, pairs, reads, writes, first=True, last=True, skip=False):
        def fn(e):
            ins = None
            n = len(pairs)
            for i, (l, r) in enumerate(pairs):
                kw = {}
                if skip:
                    kw["skip_group_check"] = True
                ins = e.matmul(out, l, r, start=(first and i == 0), stop=(last and i == n - 1), **kw)
            return ins
        S.add("pe", fn, reads, writes)

    def ps_key(b):
        return ("ps", b)

    def nT_keys(blk):
        return [("nT", 4 * blk + i) for i in range(4)]

    def wload(dst, src, key, after=()):
        dma("pool", "w_" + key, dst, src, list(after), [("w", key)])

    dma("sp", "cst", cst, cst_d, [], [("cst",)])
    S.add("pool", lambda e: e.memset(ones, 1.0), [], [("ones",)])
    S.add("pool", lambda e: e.memset(cvec[:, 0:1], EPS), [], [("cvec",)])
    S.add("pool", lambda e: e.memset(cvec[:, 1:2], 1.0), [("cvec",)], [("cvec",)])

    def load_gb(slot, src2d):
        dma("sp", "gb%d" % slot, gb[slot], src2d.partition_broadcast(128), [], [("gb", slot)])

    def norm_stage(*a, **kw):
        for _ in norm_gen(*a, **kw):
            pass

    def norm_stage(*a, **kw):
        for _ in norm_gen(*a, **kw):
            pass

    def advance(gen, n):
        for _ in range(n):
            try:
                next(gen)
            except StopIteration:
                break

    def norm_gen(n_tiles, gslot, dstT, dst_key, src_dram=None, xt=None, xn=None, junk=None, xtag="xt",
                 xnk="xn", jk="junk"):
        G = 4 if n_tiles >= 4 else n_tiles
        ngrp = n_tiles // G

        def squares(g):
            for i in range(g * G, (g + 1) * G):
                if src_dram is not None:
                    src = xt[i % len(xt)]
                    sk = [(xtag, i % len(xt))]
                    dma("sp" if i % 2 == 0 else "pool", "%s%d" % (xtag, i % len(xt)), src,
                        src_dram[i * 128:(i + 1) * 128, :], [], sk)
                else:
                    src = h[:, i, :]
                    sk = [("h", i, 0), ("h", i, 1)]
                act(junk, src, AF.Square, sk, [(jk,), ("ss", i)], accum_out=ss[:, i:i + 1])
            gs = slice(g * G, (g + 1) * G)
            gk = list(range(g * G, (g + 1) * G))
            act(ms[:, gs], ss[:, gs], AF.Ln, [("ss", i) for i in gk] + [("cvec",)], [("ms", i) for i in gk],
                scale=1.0 / D, bias=eps_ap)
            act(rstd[:, gs], ms[:, gs], AF.Exp, [("ms", i) for i in gk], [("rstd", i) for i in gk], scale=-0.5)

        if src_dram is None:
            squares(0)
        for g in range(ngrp):
            if src_dram is not None:
                squares(g)
            pend = None
            for i in range(g * G, (g + 1) * G):
                if src_dram is not None:
                    src = xt[i % len(xt)]
                    sk = [(xtag, i % len(xt))]
                else:
                    src = h[:, i, :]
                    sk = [("h", i, 0), ("h", i, 1)]
                xnb = xn[i % 2]
                stt(xnb, src, rstd[:, i:i + 1], gb[gslot], ALU.mult, ALU.mult,
                    sk + [("rstd", i), ("gb", gslot)], [(xnk, i % 2)])
                b = S.bank()
                pb = banks[b].bitcast(BF16)

                def tr(e, xnb=xnb, pb=pb):
                    ins = None
                    for c in range(8):
                        ins = e.transpose(pb[:, c * 128:(c + 1) * 128], xnb[:, c * 128:(c + 1) * 128], ident)
                    return ins
                S.add("pe", tr, [(xnk, i % 2), ("cst",)], [ps_key(b)])
                if i == g * G and src_dram is None and g + 1 < ngrp:
                    squares(g + 1)
                if pend is not None:
                    pend()

                def mk(b=b, pb=pb, i=i):
                    def f():
                        cp("act", dstT[:, :, i * 128:(i + 1) * 128], pb.rearrange("p (c n) -> p c n", c=8),
                           [ps_key(b)], [(dst_key, i)])
                    return f
                pend = mk()
            pend()
            yield

    fm = Mem(arena, [(F0, F1), (H0, H1)])
    yT = fm.alloc([8, T], BF16)
    F_REST = fm.segs[0][0]
    xt = [fm.alloc([D], F32) for _ in range(4)]
    whA_pre = arena[:, F_REST // 4:(F_REST + 16384) // 4].bitcast(BF16).rearrange("p (c n) -> p c n", c=8)
    wqkv = fm.alloc([8, 768], BF16)
    wa = fm.alloc([8, D], BF16, parts=64)
    wga = fm.alloc([8, D], BF16)
    w_in = w_in_d[0].rearrange("(c p) n -> p c n", p=128)
    wload(wqkv, w_in[:, :, 0:768], "qkv")
    xn = [fm.alloc([D], BF16), fm.alloc([D], BF16)]
    junk = fm.alloc([D], BF16)
    load_gb(0, g_mix_d)
    norm_stage(NT, 0, nT, "nT", src_dram=x_d, xt=xt, xn=xn, junk=junk)
    wload(wa, w_a_d[0].rearrange("(h d) n -> d h n", d=64), "wa", after=[("nT", 7)])
    wload(wga, w_in[:, :, 2816:3840], "wga", after=[("nT", 7)])

    def dump(name, ap, keys, shape, dt):
        d = nc.dram_tensor("dbg_" + name, list(shape), dt, kind="ExternalOutput").ap()
        if len(ap.shape) == 3:
            dv = d.rearrange("p (c n) -> p c n", c=ap.shape[1])
        else:
            dv = d
        dma("sp", "dbg", dv, ap, keys, [])

    def finish():
        S.emit(nc, stack)
        return nc, stack

    if debug == "nT":
        dump("nT", nT, [("nT", i) for i in range(NT)], [128, 8 * T], BF16)
        return finish()

    qT = [fm.alloc([8, 512], BF16), fm.alloc([8, 512], BF16)]
    kT = fm.alloc([2, T], BF16)
    vv = fm.alloc([NT, 128], BF16)
    oT = fm.alloc([8, 512], BF16, parts=64)
    pT = [fm.alloc([2, 512], BF16), fm.alloc([2, 512], BF16)]
    den = [fm.alloc([512], F32, parts=64), fm.alloc([512], F32, parts=64)]
    sg = [fm.alloc([512], F32), fm.alloc([512], F32)]
    sink_b = fm.alloc([8, 128], F32, parts=64)

    dma("sp", "misc", sinkexp, sinks_d.partition_broadcast(64), [], [("sinkexp",)])
    act(sinkexp, sinkexp, AF.Exp, [("sinkexp",)], [("sinkexp",)])
    cp("dve", sink_b, sinkexp.unsqueeze(2).to_broadcast([64, 8, 128]), [("sinkexp",)], [("sink_b",)])
    for i in range(2):
        S.add("pool", lambda e, i=i: e.memset(qT[i][64:128], 0.0), [], [("qTz", i)])
    S.add("pool", lambda e: e.memset(kT[64:128], 0.0), [], [("kTz",)])

    def biasT(g, kind):
        o = (g * 2 + kind) * 512
        return cst[:, o:o + 512]

    def qk_proj(blk_, hh, eng):
        cs_ = slice(blk_ * 512, (blk_ + 1) * 512)
        b = S.bank()
        col = hh * 64 if hh < 8 else 512 + (hh - 8) * 64
        mmg(banks[b][0:64, :], [(wqkv[:, k, col:col + 64], nT[:, k, cs_]) for k in range(8)],
            nT_keys(blk_) + [("w", "qkv")], [ps_key(b)])
        if hh < 8:
            cp(eng, qT[blk_ % 2][0:64, hh, :], banks[b][0:64, :], [ps_key(b)], [("qT", blk_ % 2, hh)])
        else:
            cp(eng, kT[0:64, hh - 8, cs_], banks[b][0:64, :], [ps_key(b)], [("kT", hh - 8, blk_)])

    QK_SPLIT = [[8, 9], [0], [1], [2, 3], [4], [5], [6], [7]]
    for blk in range(NB):
        cs = slice(blk * 512, (blk + 1) * 512)
        qb = qT[blk % 2]
        if blk == 0:
            for hh in range(10):
                qk_proj(0, hh, "act")
        def v_proj(tile):
            b = S.bank()
            mmg(banks[b][:, 0:128], [(nT[:, k, tile * 128:(tile + 1) * 128], wqkv[:, k, 640:768]) for k in range(8)],
                [("nT", tile), ("w", "qkv")], [ps_key(b)])
            cp("dve", vv[:, tile, :], banks[b][:, 0:128], [ps_key(b)], [("v", tile)])
        if blk == 0:
            for tl in range(4):
                v_proj(tl)
        if blk == 1:
            xk = [("xt", i) for i in range(4)]
            dma("pool", "w_whA0", whA_pre[:, :, 0:512], w_in[:, :, 768 + 1024:768 + 1536], [], [("w", "whA_hi")] + xk)
            dma("pool", "w_whA1", whA_pre[:, :, 512:1024], w_in[:, :, 768 + 512:768 + 1024], [("w", "whA_hi")],
                [("w", "whA_hf")])
        items = [(tl, g) for tl in range(4) for g in range(2)]

        def att_S(it_):
            tl, g = items[it_]
            pp = it_ % 2
            j = 4 * blk + tl
            kts = ([(j - 1, 0)] if j >= 1 else []) + [(j, 1)]
            rq = qb[:, 4 * g:4 * g + 4, tl * 128:(tl + 1) * 128]
            qkeys = [("qT", blk % 2, 4 * g + r) for r in range(4)] + [("qTz", blk % 2), ("kTz",)]
            for idx, (kt, kind) in enumerate(kts):
                b = S.bank()

                def fn(e, b=b, kt=kt, kind=kind, rq=rq, g=g):
                    e.matmul(banks[b].rearrange("p (a n) -> p a n", a=4), kT[:, g, kt * 128:(kt + 1) * 128], rq,
                             start=True, stop=False)
                    return e.matmul(banks[b], ident, biasT(g, kind), start=False, stop=True)
                S.add("pe", fn, qkeys + [("kT", g, kt // 4), ("cst",)], [ps_key(b)])
                act(pT[pp][:, idx, :], banks[b], AF.Exp, [ps_key(b)], [("pT", pp, idx)], scale=0.125)

        def att_PV(it_):
            tl, g = items[it_]
            pp = it_ % 2
            j = 4 * blk + tl
            kts = ([(j - 1, 0)] if j >= 1 else []) + [(j, 1)]
            pk = [("pT", pp, idx) for idx in range(len(kts))]
            bO = S.bank()
            mmg(banks[bO][0:64, :], [(vv[:, kt, g * 64:(g + 1) * 64], pT[pp][:, idx, :])
                                     for idx, (kt, kind) in enumerate(kts)],
                pk + [("v", kt) for kt, _ in kts], [ps_key(bO)])
            bD = S.bank()
            mmg(banks[bD][0:64, :], [(ones[:, 0:64], pT[pp][:, idx, :]) for idx in range(len(kts))],
                pk + [("ones",)], [ps_key(bD)])
            def lnf(e, pp=pp, bD=bD, g=g):
                ins = None
                for r in range(4):
                    ins = e.activation(out=den[pp][:, r * 128:(r + 1) * 128], in_=banks[bD][0:64, r * 128:(r + 1) * 128],
                                       func=AF.Ln, bias=sinkexp[:, 4 * g + r:4 * g + r + 1])
                return ins
            S.add("act", lnf, [ps_key(bD), ("sinkexp",)], [("den", pp)])
            act(den[pp], den[pp], AF.Exp, [("den", pp)], [("den", pp)], scale=-1.0)
            tt("dve", oT[:, 4 * g:4 * g + 4, tl * 128:(tl + 1) * 128],
               banks[bO][0:64, :].rearrange("p (a n) -> p a n", a=4),
               den[pp].rearrange("p (a n) -> p a n", a=4), ALU.mult,
               [ps_key(bO), ("den", pp)], [("oT", g, tl)])

        att_S(0)
        for it_ in range(len(items)):
            if it_ + 1 < len(items):
                att_S(it_ + 1)
            att_PV(it_)
            if blk + 1 < NB:
                for hh in QK_SPLIT[it_]:
                    qk_proj(blk + 1, hh, "dve")
                if it_ % 2 == 1:
                    v_proj(4 * (blk + 1) + it_ // 2)
        okeys = [("oT", g, tl) for g in range(2) for tl in range(4)]
        for fc in range(8):
            fs = slice(fc * 128, (fc + 1) * 128)
            bY = S.bank()
            mmg(banks[bY], [(wa[:, hh, fs], oT[:, hh, :]) for hh in range(8)], okeys + [("w", "wa")], [ps_key(bY)])
            bG = S.bank()
            mmg(banks[bG], [(wga[:, k, fs], nT[:, k, cs]) for k in range(8)],
                nT_keys(blk) + [("w", "wga")], [ps_key(bG)])
            act(sg[fc % 2], banks[bG], AF.Sigmoid, [ps_key(bG)], [("sg", fc % 2)])
            tt("dve", yT[:, fc, cs], banks[bY], sg[fc % 2], ALU.mult, [ps_key(bY), ("sg", fc % 2)], [("yT", fc, blk)])

    yT_keys = [("yT", fc, blk) for fc in range(8) for blk in range(NB)]
    if debug == "yA":
        dump("yT", yT, yT_keys, [128, 8 * T], BF16)
        return finish()

    S.barrier()
    fm = Mem(arena, [(F_REST, F1), (H0, H1)])
    whA = fm.alloc([8, 1024], BF16)
    whB = fm.alloc([8, 1024], BF16)
    wb = fm.alloc([4, D], BF16)
    wgb = fm.alloc([8, D], BF16)
    resetm = pm.alloc([512], F32)
    Scar = fm.alloc([4, 128], F32)
    vc = fm.alloc([4, 512], BF16)
    sgm = fm.alloc([4, 512], F32)
    slu = fm.alloc([4, 512], BF16)
    uT = fm.alloc([4, 512], BF16)
    sg = [fm.alloc([512], F32), fm.alloc([512], F32)]
    sgg = sg
    tmpy = [fm.alloc([512], F32), fm.alloc([512], F32)]
    HB = []
    for i in range(2):
        HB.append(dict(
            A=fm.alloc([512], F32), B=fm.alloc([512], F32), C=fm.alloc([512], F32),
            q_dec=fm.alloc([512], BF16), k_inv=fm.alloc([512], BF16), k_endT=fm.alloc([512], BF16),
            k_end=fm.alloc([4, 128], BF16), aT=fm.alloc([4, 128], BF16),
            Sfp=fm.alloc([7, 128], F32), Sb=fm.alloc([8, 128], BF16), decay=fm.alloc([8, 1], F32)))

    dma("pool", "w_whB0", whB[:, :, 0:512], w_in[:, :, 768 + 1536:768 + 2048], [], [("w", "whB_hg")])
    dma("pool", "w_whB1", whB[:, :, 512:1024], w_in[:, :, 768:768 + 512], [("w", "whB_hg")], [("w", "whB_hq")])
    wload(wgb, w_in[:, :, 3840:4864], "wgb", after=[("w", "whB_hq")])
    wload(wb, w_b_d[0].rearrange("(h v) n -> v h n", v=128), "wb", after=[("w", "whB_hq")])
    dma("sp", "misc", l0T, lb_d[0].rearrange("(h p) -> p h", p=128), [], [("l0T",)], allow_slow_non_contiguous=True)
    dma("sp", "misc", l1T, lb_d[1].rearrange("(h p) -> p h", p=128), [], [("l1T",)], allow_slow_non_contiguous=True)
    dma("sp", "misc", gon, g_on_d[0].rearrange("(p o) -> p o", o=1), [], [("gon",)])
    tt("dve", omlb, l1T, l0T, ALU.subtract, [("l0T",), ("l1T",)], [("omlb",)])
    act(omlb, omlb, AF.Sigmoid, [("omlb",)], [("omlb",)])
    ts("dve", nomlb, omlb, -1.0, None, ALU.mult, ALU.bypass, [("omlb",)], [("nomlb",)])
    S.add("pool", lambda e: e.memset(resetm, 1.0), [], [("resetm",)])
    S.add("pool", lambda e: e.memset(resetm.rearrange("p (c s) -> p c s", s=64)[:, :, 0:1], 0.0),
          [("resetm",)], [("resetm",)])
    S.add("pool", lambda e: e.memset(Scar, 0.0), [], [("Scar", hh) for hh in range(4)])

    def hgrn_stages(blk, hh):
        cs = slice(blk * 512, (blk + 1) * 512)
        s = hh % 2
        X = HB[s]
        A, Bf, C = X["A"], X["B"], X["C"]
        q_dec, k_inv, k_endT, k_end, aT = X["q_dec"], X["k_inv"], X["k_endT"], X["k_end"], X["aT"]
        osq = k_endT
        Sfp, Sb, decay = X["Sfp"], X["Sb"], X["decay"]

        def K(n):
            return (n, s)
        act(A, sgm[:, hh, :], AF.Ln, [("sgm", hh), ("nomlb",), ("cvec",)], [K("A")],
            scale=nomlb[:, hh:hh + 1], bias=one_ap)
        bQ = S.bank()
        mmg(banks[bQ], [(whB[:, k, 512 + hh * 128:512 + (hh + 1) * 128], nT[:, k, cs]) for k in range(8)],
            nT_keys(blk) + [("w", "whB_hq")], [ps_key(bQ)])
        yield
        S.add("dve", lambda e: e.tensor_tensor_scan(out=A, data0=resetm, data1=A, initial=0.0,
                                                    op0=ALU.mult, op1=ALU.add),
              [("resetm",), K("A")], [K("A")])
        yield
        act(Bf, A, AF.Exp, [K("A")], [K("B")])
        act(C, A, AF.Exp, [K("A")], [K("C")], scale=-1.0)
        act(decay, A.rearrange("p (c s) -> p c s", s=64)[:, :, 63:64], AF.Exp, [K("A")], [K("decay")])
        yield
        stt(q_dec, banks[bQ], float(128 ** -0.5), Bf, ALU.mult, ALU.mult, [ps_key(bQ), K("B")], [K("q_dec")])
        stt(k_inv, sgm[:, hh, :], omlb[:, hh:hh + 1], C, ALU.mult, ALU.mult,
            [("sgm", hh), ("omlb",), K("C")], [K("k_inv")])
        yield
        tt("dve", k_endT.rearrange("p (c s) -> p c s", s=64), k_inv.rearrange("p (c s) -> p c s", s=64),
           decay.to_broadcast([128, 8, 64]), ALU.mult, [K("k_inv"), K("decay")], [K("k_endT")])
        yield
        bT_ = S.bank()
        pbT = banks[bT_].bitcast(BF16)

        def trk(e):
            ins = None
            for tl in range(4):
                ins = e.transpose(pbT[:, tl * 128:(tl + 1) * 128], k_endT[:, tl * 128:(tl + 1) * 128], ident)
            return ins
        S.add("pe", trk, [K("k_endT"), ("cst",)], [ps_key(bT_)])
        bA = S.bank()

        def af(e):
            ins = None
            for tl in range(4):
                sl = slice(tl * 128, (tl + 1) * 128)
                ins = e.matmul(banks[bA][:, sl], k_inv[:, sl], q_dec[:, sl], start=True, stop=True)
            return ins
        S.add("pe", af, [K("k_inv"), K("q_dec")], [ps_key(bA)])
        yield
        cp("act", k_end.rearrange("p a n -> p (a n)"), pbT[:, 0:512], [ps_key(bT_)], [K("k_end")])
        tt("dve", aT, banks[bA].rearrange("p (a n) -> p a n", a=4), maskT.unsqueeze(1).to_broadcast([128, 4, 128]),
           ALU.mult, [ps_key(bA), ("cst",)], [K("aT")])
        cp("pool", Sb[:, 0, :], Scar[:, hh, :], [("Scar", hh)], [K("Sb0")])
        yield
        bD = [S.bank(), S.bank()]

        def dsf(e):
            ins = None
            for n in range(8):
                tl, half = n // 2, n % 2
                rows = slice(half * 64, half * 64 + 64)
                ins = e.matmul(banks[bD[half]][:, tl * 128:(tl + 1) * 128], k_end[rows, tl, :],
                               vc[rows, tl, hh * 128:(hh + 1) * 128], start=True, stop=True)
            return ins
        S.add("pe", dsf, [K("k_end")] + [("vc", tl) for tl in range(4)], [ps_key(bD[0]), ps_key(bD[1])])
        yield
        for n in range(8):
            src = Scar[:, hh, :] if n == 0 else Sfp[:, n - 1, :]
            dst = Scar[:, hh, :] if n == 7 else Sfp[:, n, :]
            rk = [("Scar", hh)] if n == 0 else [K("Sfp%d" % (n - 1))]
            wk = [("Scar", hh)] if n == 7 else [K("Sfp%d" % n)]
            stt(dst, src, decay[:, n, :], banks[bD[n % 2]][:, (n // 2) * 128:(n // 2 + 1) * 128],
                ALU.mult, ALU.add, rk + [K("decay"), ps_key(bD[n % 2])], wk)
            if n == 3:
                yield
        cp("dve", Sb[:, 1:8, :], Sfp[:, 0:7, :], [K("Sfp%d" % n) for n in range(7)], [K("Sb1")])
        yield
        bO = S.bank()

        def of(e):
            ins = None
            for tl in range(4):
                sl = slice(tl * 128, (tl + 1) * 128)
                e.matmul(banks[bO][:, sl], vc[:, tl, hh * 128:(hh + 1) * 128], aT[:, tl, :], start=True, stop=False,
                         skip_group_check=True)
                for cc in range(2):
                    n = 2 * tl + cc
                    s2 = slice(tl * 128 + cc * 64, tl * 128 + cc * 64 + 64)
                    ins = e.matmul(banks[bO][:, s2], Sb[:, n, :], q_dec[:, s2], start=False, stop=(cc == 1),
                                   skip_group_check=True)
            return ins
        S.add("pe", of, [K("aT"), K("q_dec"), K("Sb0"), K("Sb1")] + [("vc", tl) for tl in range(4)], [ps_key(bO)])
        yield
        act(osq, banks[bO], AF.Square, [ps_key(bO)], [K("k_endT")])
        yield
        bN = S.bank()
        mmg(banks[bN], [(ones, osq)], [K("k_endT"), ("ones",)], [ps_key(bN)])
        yield
        act(Bf, banks[bN], AF.Ln, [ps_key(bN), ("cvec",)], [K("B")], scale=1.0 / 128, bias=eps_ap)
        act(Bf, Bf, AF.Exp, [K("B")], [K("B")], scale=-0.5)
        yield
        tt("dve", C, banks[bO], Bf, ALU.mult, [ps_key(bO), K("B")], [K("C")])
        tt("dve", uT[:, hh, :], C, slu[:, hh, :], ALU.mult, [K("C"), ("slu", hh)], [("uT", hh)])
        yield

    sgB = gb[0][:, :].bitcast(BF16)
    sgB2 = gb[1][:, :].bitcast(BF16)

    def sgB_ap(fc):
        src = sgB if fc < 4 else sgB2
        return src[:, (fc % 4) * 512:(fc % 4 + 1) * 512]

    def sig_from_psum(out, bank, neg, tmp, tkey, rkeys, wkeys):
        act(tmp, banks[bank], AF.Exp, [ps_key(bank)] + rkeys, [tkey], scale=(1.0 if neg else -1.0))
        act(tmp, tmp, AF.Ln, [tkey, ("cvec",)], [tkey], bias=one_ap)
        act(out, tmp, AF.Exp, [tkey], wkeys, scale=-1.0)

    act_next = []

    def p1_chunks(blk, hh):
        cs = slice(blk * 512, (blk + 1) * 512)

        def c_f():
            bF = S.bank()
            mmg(banks[bF], [(whA[:, k, 512 + hh * 128:512 + (hh + 1) * 128], nT[:, k, cs]) for k in range(8)],
                nT_keys(blk) + [("w", "whA_hf")], [ps_key(bF)])
            act_next.append(lambda: sig_from_psum(sgm[:, hh, :], bF, True, sg[0], ("sg", 0), [], [("sgm", hh)]))

        def c_g():
            bG = S.bank()
            mmg(banks[bG], [(whB[:, k, hh * 128:(hh + 1) * 128], nT[:, k, cs]) for k in range(8)],
                nT_keys(blk) + [("w", "whB_hg")], [ps_key(bG)])

            def a():
                sig_from_psum(sg[1], bG, False, sg[1], ("sg", 1), [], [("sg", 1)])
                deferred.append(lambda: stt(slu[:, hh, :], banks[bG], gon, sg[1], ALU.mult, ALU.mult,
                                            [ps_key(bG), ("sg", 1), ("gon",)], [("slu", hh)]))
            act_next.append(a)
        return [c_f, c_g]

    def gate_chunk(blk, fc):
        cs = slice(blk * 512, (blk + 1) * 512)

        def c():
            bG = S.bank()
            mmg(banks[bG], [(wgb[:, k, fc * 128:(fc + 1) * 128], nT[:, k, cs]) for k in range(8)],
                nT_keys(blk) + [("w", "wgb")], [ps_key(bG)])
            act_next.append(lambda: sig_from_psum(sgB_ap(fc), bG, False, sg[fc % 2], ("sg", fc % 2), [],
                                                  [("sgB", fc)]))
        return c

    deferred = []

    def flush_deferred():
        n = len(deferred)
        for _ in range(n):
            deferred.pop(0)()
        n = len(act_next)
        for _ in range(n):
            act_next.pop(0)()

    def run_pair(blk, pair, fillers):
        gens = [hgrn_stages(blk, 2 * pair), hgrn_stages(blk, 2 * pair + 1)]
        alive = True
        rnd = 0
        while alive:
            alive = False
            for g_ in gens:
                try:
                    next(g_)
                    alive = True
                except StopIteration:
                    pass
            rnd += 1
            flush_deferred()
            if fillers and rnd >= 2:
                fillers.pop(0)()
        while fillers:
            flush_deferred()
            fillers.pop(0)()
        flush_deferred()
        flush_deferred()

    nreg = Mem(arena, [(N0, N1)])
    wmo = nreg.alloc([8, D], BF16)
    wckK = nreg.alloc([8, D], BF16)
    w_mo = w_mo_d[0].rearrange("(c p) n -> p c n", p=128)

    def wmo_prefetch():
        dma("pool", "w_wmo", wmo, w_mo, [], [("w", "wmo")] + [("nT", i) for i in range(NT)])

    for blk in range(NB):
        cs = slice(blk * 512, (blk + 1) * 512)
        for tl in range(4):
            tile = 4 * blk + tl
            b = S.bank()
            mmg(banks[b], [(nT[:, k, tile * 128:(tile + 1) * 128], whA[:, k, 0:512]) for k in range(8)],
                [("nT", tile), ("w", "whA_hi")], [ps_key(b)])
            cp("dve", vc[:, tl, :], banks[b], [ps_key(b)], [("vc", tl)])
        if blk == 0:
            pc = [p1_chunks(0, hh) for hh in range(2)]
            for c in (pc[0][0], pc[1][0], pc[0][1], pc[1][1]):
                c()
                flush_deferred()
            flush_deferred()
            flush_deferred()
        f0 = [gate_chunk(blk, fc) for fc in range(4)]
        p1 = p1_chunks(blk, 2) + p1_chunks(blk, 3)
        f0 = [f0[0], p1[0], f0[1], p1[1], f0[2], p1[2], f0[3], p1[3]]
        run_pair(blk, 0, f0)
        f1 = [gate_chunk(blk, fc) for fc in range(4, 8)]
        if blk + 1 < NB:
            p1 = p1_chunks(blk + 1, 0) + p1_chunks(blk + 1, 1)
            f1 = [f1[0], p1[0], f1[1], p1[1], f1[2], p1[2], f1[3], p1[3]]
        if blk == NB - 1:
            f1 = f1 + [wmo_prefetch]
        run_pair(blk, 1, f1)
        for fc in range(8):
            fs = slice(fc * 128, (fc + 1) * 128)
            bY = S.bank()
            mmg(banks[bY], [(wb[:, hh, fs], uT[:, hh, :]) for hh in range(4)],
                [("uT", hh) for hh in range(4)] + [("w", "wb")], [ps_key(bY)])
            tt("dve", tmpy[fc % 2], banks[bY], sgB_ap(fc), ALU.mult, [ps_key(bY), ("sgB", fc)],
               [("tmpy", fc % 2)])
            tt("pool", yT[:, fc, cs], yT[:, fc, cs], tmpy[fc % 2], ALU.add, [("yT", fc, blk), ("tmpy", fc % 2)],
               [("yT", fc, blk)])

    if debug == "yB":
        dump("yT", yT, yT_keys, [128, 8 * T], BF16)
        return finish()

    S.barrier()
    fm = Mem(arena, [(F_REST, F1)])
    xt = [fm.alloc([D], F32), fm.alloc([D], F32)]
    wckV = fm.alloc([8, D], BF16)
    KV0 = fm.segs[0][0]
    KT = fm.alloc([8, 256], BF16)
    Vm = fm.alloc([2, D], BF16)
    KV1 = fm.segs[0][0]
    nmT = fm.alloc([8, 256], BF16)
    xtm = [fm.alloc([D], F32), fm.alloc([D], F32)]
    xnm = [fm.alloc([D], BF16), fm.alloc([D], BF16)]
    junkm = fm.alloc([D], BF16)
    load_gb(0, g_mem_d)
    w_ckv = w_ckv_d[0].rearrange("(c p) n -> p c n", p=128)
    wload(wckK, w_ckv[:, :, 0:D], "wckK")
    wload(wckV, w_ckv[:, :, D:2 * D], "wckV", after=[("w", "wckK")])

    def mem_kv():
        norm_stage(2, 0, nmT, "nmT", src_dram=mem_d, xt=xtm, xn=xnm, junk=junkm, xtag="xm", xnk="xnm", jk="junkm")
        nm_keys = [("nmT", 0), ("nmT", 1)]
        for c8 in range(8):
            b = S.bank()
            mmg(banks[b][:, 0:256], [(wckK[:, k, c8 * 128:(c8 + 1) * 128], nmT[:, k, :]) for k in range(8)],
                nm_keys + [("w", "wckK")], [ps_key(b)])
            cp("act", KT[:, c8, :], banks[b][:, 0:256], [ps_key(b)], [("KT", c8)])
        for mt in range(2):
            for half in range(2):
                b = S.bank()
                mmg(banks[b], [(nmT[:, k, mt * 128:(mt + 1) * 128], wckV[:, k, half * 512:(half + 1) * 512])
                               for k in range(8)], nm_keys + [("w", "wckV")], [ps_key(b)])
                cp("act", Vm[:, mt, half * 512:(half + 1) * 512], banks[b], [ps_key(b)], [("Vm", mt, half)])

    for tile in range(NT):
        tsl = slice(tile * 128, (tile + 1) * 128)
        if tile == 10 and debug != "hB":
            mem_kv()
        dma("sp", "xt%d" % (tile % 2), xt[tile % 2], x_d[tsl, :], [], [("xt", tile % 2)])
        for half in range(2):
            hs = slice(half * 512, (half + 1) * 512)
            b = S.bank()
            mmg(banks[b], [(yT[:, k, tsl], wmo[:, k, hs]) for k in range(8)],
                [("yT", k, tile // 4) for k in range(8)] + [("w", "wmo")], [ps_key(b)])
            tt("dve", h[:, tile, hs], banks[b], xt[tile % 2][:, hs], ALU.add, [ps_key(b), ("xt", tile % 2)],
               [("h", tile, half)])

    h_keys = [("h", i, hf) for i in range(NT) for hf in range(2)]
    if debug == "hB":
        dump("h", h, h_keys, [128, NT * D], F32)
        return finish()


    S.barrier()
    fm = Mem(arena, [(F0, KV0), (KV1, F1)])
    xn = [fm.alloc([D], BF16), fm.alloc([D], BF16)]
    junk = fm.alloc([D], BF16)
    load_gb(1, g_cross_d)
    wcq = fm.alloc([8, D], BF16)
    wco = fm.alloc([8, D], BF16)
    qTc2 = [fm.alloc([8, 512], BF16), fm.alloc([8, 512], BF16)]
    oTc = fm.alloc([8, 512], BF16)
    pTc = [fm.alloc([2, 512], BF16), fm.alloc([2, 512], BF16)]
    rec = [fm.alloc([512], F32), fm.alloc([512], F32)]
    wload(wcq, w_cq_d[0].rearrange("(c p) n -> p c n", p=128), "wcq")
    wload(wco, w_co_d[0].rearrange("(c p) n -> p c n", p=128), "wco", after=[("w", "wcq")])
    ngen = norm_gen(NT, 1, nT, "nT", xn=xn, junk=junk)
    def qproj(blk_, c8, eng):
        cs_ = slice(blk_ * 512, (blk_ + 1) * 512)
        b = S.bank()
        mmg(banks[b], [(wcq[:, k, c8 * 128:(c8 + 1) * 128], nT[:, k, cs_]) for k in range(8)],
            nT_keys(blk_) + [("w", "wcq")], [ps_key(b)])
        cp(eng, qTc2[blk_ % 2][:, c8, :], banks[b], [ps_key(b)], [("qTc", blk_ % 2, c8)])

    advance(ngen, 2)
    for c8 in range(8):
        qproj(0, c8, "act" if c8 % 2 else "dve")
    for blk in range(NB):
        cs = slice(blk * 512, (blk + 1) * 512)
        qTc = qTc2[blk % 2]
        if blk + 1 < NB:
            advance(ngen, 1)
        def x_S(head):
            pp = head % 2
            for mt in range(2):
                b = S.bank()
                mmg(banks[b], [(KT[:, head * 2 + dc, mt * 128:(mt + 1) * 128], qTc[:, head * 2 + dc, :]) for dc in range(2)],
                    [("KT", head * 2), ("KT", head * 2 + 1), ("qTc", blk % 2, head * 2), ("qTc", blk % 2, head * 2 + 1)],
                    [ps_key(b)])
                act(pTc[pp][:, mt, :], banks[b], AF.Exp, [ps_key(b)], [("pTc", pp, mt)], scale=1.0 / 16)

        def x_PV(head):
            pp = head % 2
            pk = [("pTc", pp, 0), ("pTc", pp, 1)]
            bD = S.bank()
            mmg(banks[bD], [(ones, pTc[pp][:, mt, :]) for mt in range(2)], pk + [("ones",)], [ps_key(bD)])
            act(rec[pp], banks[bD], AF.Ln, [ps_key(bD)], [("rec", pp)])
            act(rec[pp], rec[pp], AF.Exp, [("rec", pp)], [("rec", pp)], scale=-1.0)
            for dc in range(2):
                bO = S.bank()
                c0 = head * 256 + dc * 128
                mmg(banks[bO], [(Vm[:, mt, c0:c0 + 128], pTc[pp][:, mt, :]) for mt in range(2)],
                    pk + [("Vm", mt, c0 // 512) for mt in range(2)], [ps_key(bO)])
                tt("dve", oTc[:, head * 2 + dc, :], banks[bO], rec[pp], ALU.mult, [ps_key(bO), ("rec", pp)],
                   [("oTc", head * 2 + dc)])

        x_S(0)
        for head in range(4):
            if head + 1 < 4:
                x_S(head + 1)
            x_PV(head)
            if blk + 1 < NB:
                qproj(blk + 1, 2 * head, "dve")
                qproj(blk + 1, 2 * head + 1, "dve")
        for tl in range(4):
            tile = 4 * blk + tl
            for half in range(2):
                hs = slice(half * 512, (half + 1) * 512)
                b = S.bank()
                mmg(banks[b], [(oTc[:, k, tl * 128:(tl + 1) * 128], wco[:, k, hs]) for k in range(8)],
                    [("oTc", k) for k in range(8)] + [("w", "wco")], [ps_key(b)])
                tt("dve", h[:, tile, hs], banks[b], h[:, tile, hs], ALU.add, [ps_key(b), ("h", tile, half)],
                   [("h", tile, half)])

    if debug == "hC":
        dump("h", h, h_keys, [128, NT * D], F32)
        return finish()

    S.barrier()
    fm = Mem(arena, [(F0, F1)])
    wfi = [fm.alloc([8, 1024], BF16), fm.alloc([8, 1024], BF16)]
    wfd = [fm.alloc([4, D], BF16), fm.alloc([4, D], BF16)]
    actT = fm.alloc([4, T], BF16)
    ub = [fm.alloc([520], F32), fm.alloc([520], F32)]
    cbuf = [fm.alloc([512], F32), fm.alloc([512], F32)]
    sbf = [fm.alloc([512], F32), fm.alloc([512], F32)]
    xnbuf = fm.alloc([D], F32)
    xn = [xnbuf[:, 0:512].bitcast(BF16), xnbuf[:, 512:1024].bitcast(BF16)]
    junk = fm.alloc([D], BF16)
    load_gb(0, g_ffn_d)
    load_gb(1, g_fin_d.rearrange("(o n) -> o n", o=1))
    ot = [fm.alloc([D], F32), xnbuf]
    for w in range(3):
        dma("sp", "misc", cwT[:, :, w], cw_d[0, w].rearrange("(j p) -> p j", p=128), [], [("cwT", w)],
            allow_slow_non_contiguous=True)
    dma("sp", "misc", cbT, cb_d[0].rearrange("(j p) -> p j", p=128), [], [("cbT",)], allow_slow_non_contiguous=True)
    w_fi = w_fi_d[0].rearrange("(c p) n -> p c n", p=128)
    groups = [(0, 2), (2, 4), (6, 4), (10, 4), (14, 4), (18, 4)]

    def ffn_load(gi):
        j0, nj = groups[gi]
        ws = gi % 2
        wload(wfi[ws][:, :, 0:nj * 128], w_fi[:, :, j0 * 128:(j0 + nj) * 128], "wfiu%d" % ws)
        wload(wfi[ws][:, :, 512:512 + nj * 128], w_fi[:, :, DFF + j0 * 128:DFF + (j0 + nj) * 128], "wfig%d" % ws)
        wload(wfd[ws][:, 0:nj, :], w_fd_d[0][j0 * 128:(j0 + nj) * 128, :].rearrange("(j p) n -> p j n", p=128),
              "wfd%d" % ws, after=[("w", "wfiu%d" % ws), ("w", "wfig%d" % ws)])

    fin_pend = [None]

    def final_norm_tile(i):
        src = h[:, i, :]
        skeys = [("h", i, 0), ("h", i, 1)]
        act(junk, src, AF.Square, skeys, [("junk",), ("ss", i)], accum_out=ss[:, i:i + 1])
        act(ms[:, i:i + 1], ss[:, i:i + 1], AF.Ln, [("ss", i), ("cvec",)], [("ms", i)], scale=1.0 / D, bias=eps_ap)
        act(rstd[:, i:i + 1], ms[:, i:i + 1], AF.Exp, [("ms", i)], [("rstd", i)], scale=-0.5)
        if fin_pend[0] is not None:
            fin_pend[0]()

        def second():
            stt(ot[i % 2], src, rstd[:, i:i + 1], gb[1], ALU.mult, ALU.mult,
                skeys + [("rstd", i), ("gb", 1)], [("ot", i % 2)])
            dma("sp", "out%d" % (i % 2), out_d[i * 128:(i + 1) * 128, :], ot[i % 2], [("ot", i % 2)], [])
        fin_pend[0] = second
        if i == NT - 1:
            second()
            fin_pend[0] = None

    ffn_load(0)
    ngen = norm_gen(NT, 0, nT, "nT", xn=xn, junk=junk)
    cwk = [("cwT", w) for w in range(3)] + [("cbT",)]
    it = 0
    pendB = [None]
    for gi, (j0, nj) in enumerate(groups):
        ws = gi % 2
        if gi + 1 < len(groups):
            ffn_load(gi + 1)
        for jj in range(nj):
            j = j0 + jj
            for blk in range(NB):
                cs = slice(blk * 512, (blk + 1) * 512)
                cur, prv = it % 2, (it + 1) % 2
                if gi == 0 and jj == 0:
                    advance(ngen, 2 if blk == 0 else 1)
                bU = S.bank()
                mmg(banks[bU], [(wfi[ws][:, k, jj * 128:(jj + 1) * 128], nT[:, k, cs]) for k in range(8)],
                    nT_keys(blk) + [("w", "wfiu%d" % ws)], [ps_key(bU)])
                bG = S.bank()
                mmg(banks[bG], [(wfi[ws][:, k, 512 + jj * 128:512 + (jj + 1) * 128], nT[:, k, cs]) for k in range(8)],
                    nT_keys(blk) + [("w", "wfig%d" % ws)], [ps_key(bG)])
                if blk == 0:
                    S.add("pool", lambda e, cur=cur: e.memset(ub[cur][:, 0:2], 0.0), [], [("ubh", cur)])
                else:
                    cp("pool", ub[cur][:, 0:2], ub[prv][:, 512:514], [("ub", prv)], [("ubh", cur)])
                cp("act", ub[cur][:, 2:514], banks[bU], [ps_key(bU)], [("ub", cur)])
                ts("pool", cbuf[cur], ub[cur][:, 2:514], cwT[:, j, 2:3], cbT[:, j:j + 1], ALU.mult, ALU.add,
                   [("ub", cur)] + cwk, [("cbuf", cur)])
                stt(cbuf[cur], ub[cur][:, 1:513], cwT[:, j, 1:2], cbuf[cur], ALU.mult, ALU.add,
                    [("ub", cur), ("ubh", cur), ("cbuf", cur)] + cwk, [("cbuf", cur)])
                stt(cbuf[cur], ub[cur][:, 0:512], cwT[:, j, 0:1], cbuf[cur], ALU.mult, ALU.add,
                    [("ub", cur), ("ubh", cur), ("cbuf", cur)] + cwk, [("cbuf", cur)])
                if pendB[0] is not None:
                    pendB[0]()

                def mkB(cur=cur, bG=bG, jj=jj, cs=cs, blk=blk):
                    def f():
                        act(sbf[cur], cbuf[cur], AF.Silu, [("cbuf", cur)], [("sbf", cur)])
                        tt("dve", actT[:, jj, cs], sbf[cur], banks[bG], ALU.mult, [("sbf", cur), ps_key(bG)],
                           [("actT", jj, blk)])
                    return f
                pendB[0] = mkB()
                it += 1
        if pendB[0] is not None:
            pendB[0]()
            pendB[0] = None
        for tile in range(NT):
            tsl = slice(tile * 128, (tile + 1) * 128)
            for half in range(2):
                hs = slice(half * 512, (half + 1) * 512)
                b = S.bank()
                mmg(banks[b], [(actT[:, jj, tsl], wfd[ws][:, jj, hs]) for jj in range(nj)],
                    [("actT", jj, tile // 4) for jj in range(nj)] + [("w", "wfd%d" % ws)], [ps_key(b)])
                tt("dve", h[:, tile, hs], banks[b], h[:, tile, hs], ALU.add, [ps_key(b), ("h", tile, half)],
                   [("h", tile, half)])
            if gi == len(groups) - 1 and debug != "hD":
                final_norm_tile(tile)

    if debug == "hD":
        dump("h", h, h_keys, [128, NT * D], F32)
        return finish()

    return finish()


_CACHE = {}


def kernel(**inputs):
    if "nc" not in _CACHE:
        _CACHE["nc"] = build()
    nc, _stack = _CACHE["nc"]
    cst = _consts()
    x = np.asarray(inputs["x"], dtype=np.float32)
    mem = np.asarray(inputs["mem"], dtype=np.float32)
    shared = {k: np.ascontiguousarray(np.asarray(v, dtype=np.float32)) for k, v in inputs.items()
              if k not in ("x", "mem")}
    in_maps = []
    for b in range(8):
        m = dict(shared)
        m["x"] = np.ascontiguousarray(x[b])
        m["mem"] = np.ascontiguousarray(mem[b])
        m["cst"] = cst
        in_maps.append(m)
    res = run_bass_kernel_spmd(nc, in_maps, core_ids=list(range(8)))
    return np.stack([np.asarray(r["out"], dtype=np.float32) for r in res.results], axis=0)
```
